# Optimizing a Trainium2 kernel written in Bass

```python
import math
import jax
import jax.numpy as jnp
from jax import lax
import numpy as np

D_MODEL = 1024
BATCH = 2
SEQ = 16384
DEPTH = 2
DEC_BATCH = 8
DEC_SEQ = 16
PAST_LEN = 4096

CHUNK = 64
Q_BLOCK = 128
N_EVEN = (DEPTH + 1) // 2
N_ODD = DEPTH // 2
MIX_WIDTH = D_MODEL
S5_WIDTH = MIX_WIDTH // 2
S5_GROUP_CH = 16
S5_GROUPS = S5_WIDTH // S5_GROUP_CH
S5_STATE = 64
SB_WIDTH = MIX_WIDTH - S5_WIDTH
SB_HEAD_DIM = 64
SB_HEADS = SB_WIDTH // SB_HEAD_DIM
HG_WIDTH = MIX_WIDTH // 2
HG_DK = 128
HG_DV = 128
HG_HEADS = HG_WIDTH // HG_DK
FOX_WIDTH = MIX_WIDTH - HG_WIDTH
FOX_HEAD_DIM = 64
FOX_HEADS = FOX_WIDTH // FOX_HEAD_DIM
D_FF = 2816
CONV_W = 3
NORM_EPS = 1e-6

kernel_name = 'streaming_hybrid_s5_sb_hgrn2_fox_step'

STATE_KEYS = ('sb_k', 'sb_v', 's5_re', 's5_im', 'fox_k', 'fox_v', 'fox_logf', 'hgrn', 'conv')


def rmsnorm(x, g):
    x32 = x.astype(jnp.float32)
    y = x32 * lax.rsqrt(jnp.mean(x32 * x32, axis=-1, keepdims=True) + NORM_EPS)
    return (y * g.astype(jnp.float32)).astype(x.dtype)


def attend(fn, q_arrays, q_pos):
    T = q_pos.shape[0]
    if T <= Q_BLOCK:
        return fn(q_arrays, q_pos)
    nb = T // Q_BLOCK
    blocks = tuple(a.reshape(a.shape[0], nb, Q_BLOCK, *a.shape[2:]).swapaxes(0, 1) for a in q_arrays)
    out = lax.map(lambda args: fn(args[0], args[1]), (blocks, q_pos.reshape(nb, Q_BLOCK)))
    out = out.swapaxes(0, 1)
    return out.reshape(out.shape[0], T, *out.shape[3:])


def sb_block(q, k, v, q_pos, k_pos):
    z = jnp.einsum('bqhd,bkhd->bhqk', q, k).astype(jnp.float32) * (SB_HEAD_DIM ** -0.5)
    mask = k_pos[None, :] < q_pos[:, None]
    log_stay = jnp.where(mask, jax.nn.log_sigmoid(-z), 0.0)
    log_after = lax.cumsum(log_stay, axis=3, reverse=True) - log_stay
    w = jnp.where(mask, jnp.exp(jax.nn.log_sigmoid(z) + log_after), 0.0)
    return jnp.einsum('bhqk,bkhd->bqhd', w.astype(v.dtype), v)


def fox_block(q, c_q, k, v, c_k, q_pos, k_pos):
    s = jnp.einsum('bqhd,bkhd->bhqk', q, k).astype(jnp.float32) * (FOX_HEAD_DIM ** -0.5)
    s = s + (c_q.transpose(0, 2, 1)[..., :, None] - c_k.transpose(0, 2, 1)[..., None, :])
    mask = k_pos[None, :] <= q_pos[:, None]
    p = jax.nn.softmax(jnp.where(mask, s, -jnp.inf), axis=-1)
    return jnp.einsum('bhqk,bkhd->bqhd', p.astype(v.dtype), v)


def s5_mixer(u, lam_re, lam_im, log_dt, b_re, b_im, c_re, c_im, d, w_glu, b_glu, h0):
    f32 = jnp.float32
    bsz, T, _ = u.shape
    L = min(CHUNK, T)
    lam = lax.complex(lam_re.astype(f32), lam_im.astype(f32))
    dt = jnp.exp(log_dt.astype(f32))[:, None]
    lam_bar = jnp.exp(lam * dt)
    b_bar = ((lam_bar - 1.0) / lam)[..., None] * lax.complex(b_re.astype(f32), b_im.astype(f32))
    c_mat = lax.complex(c_re.astype(f32), c_im.astype(f32))
    u_blocks = u.astype(f32).reshape(bsz, T // L, L, S5_GROUPS, S5_GROUP_CH).swapaxes(0, 1)
    a_seq = jnp.broadcast_to(lam_bar, (bsz, L, S5_GROUPS, S5_STATE))

    def combine(e1, e2):
        a1, b1 = e1
        a2, b2 = e2
        return a1 * a2, a2 * b1 + b2

    def step(h, u_blk):
        bu = jnp.einsum('gpc,blgc->blgp', b_bar, u_blk.astype(jnp.complex64))
        bu = bu.at[:, 0].add(lam_bar * h)
        _, hs = lax.associative_scan(combine, (a_seq, bu), axis=1)
        y = jnp.einsum('gcp,blgp->blgc', c_mat, hs).real
        return hs[:, -1], y

    h_last, ys = lax.scan(step, h0, u_blocks)
    y = ys.swapaxes(0, 1).reshape(bsz, T, S5_WIDTH) + d.astype(f32) * u.astype(f32)
    y = jax.nn.gelu(y)
    y = y * jax.nn.sigmoid(y @ w_glu.astype(f32) + b_glu.astype(f32))
    return y.astype(u.dtype), h_last


def hgrn2_recurrence(q, k, v, log_f, S0):
    bsz, T = q.shape[:2]
    L = min(CHUNK, T)
    n = T // L

    def blk(a):
        return a.reshape(bsz, n, L, *a.shape[2:]).swapaxes(0, 1)

    tri = jnp.tril(jnp.ones((L, L), dtype=bool))

    def step(S, inp):
        qc, kc, vc, gc = inp
        b = jnp.cumsum(gc, axis=1)
        o_inter = jnp.einsum('blhk,bhkv->blhv', qc * jnp.exp(b), S)
        diff = b[:, :, None] - b[:, None, :]
        decay = jnp.exp(jnp.where(tri[None, :, :, None, None], diff, -jnp.inf))
        scores = jnp.einsum('bthk,btshk->bths', qc, decay * kc[:, None])
        o_intra = jnp.einsum('bths,bshv->bthv', scores, vc)
        b_end = b[:, -1]
        S_new = jnp.exp(b_end)[..., None] * S + jnp.einsum('bshk,bshv->bhkv', kc * jnp.exp(b_end[:, None] - b), vc)
        return S_new, o_inter + o_intra

    S_last, o = lax.scan(step, S0, (blk(q), blk(k), blk(v), blk(log_f)))
    return o.swapaxes(0, 1).reshape(bsz, T, HG_HEADS, HG_DV), S_last


def even_mixer(h, p, i, cache):
    f32 = jnp.float32
    bsz, T, _ = h.shape
    proj = h @ p['w_in_a'][i]
    u, q, k, v = jnp.split(proj, [S5_WIDTH, S5_WIDTH + SB_WIDTH, S5_WIDTH + 2 * SB_WIDTH], axis=-1)
    q = q.reshape(bsz, T, SB_HEADS, SB_HEAD_DIM)
    k = k.reshape(bsz, T, SB_HEADS, SB_HEAD_DIM)
    v = v.reshape(bsz, T, SB_HEADS, SB_HEAD_DIM)
    if cache is None:
        h0 = jnp.zeros((bsz, S5_GROUPS, S5_STATE), jnp.complex64)
        k_all, v_all, past = k, v, 0
    else:
        h0 = lax.complex(cache['s5_re'][i].astype(f32), cache['s5_im'][i].astype(f32))
        k_all = jnp.concatenate([cache['sb_k'][i], k], axis=1)
        v_all = jnp.concatenate([cache['sb_v'][i], v], axis=1)
        past = cache['sb_k'].shape[2]
    s5_out, h_last = s5_mixer(u, p['s5_lambda_re'][i], p['s5_lambda_im'][i], p['s5_log_dt'][i],
                              p['s5_b_re'][i], p['s5_b_im'][i], p['s5_c_re'][i], p['s5_c_im'][i],
                              p['s5_d'][i], p['s5_w_glu'][i], p['s5_b_glu'][i], h0)
    q_pos = past + jnp.arange(T)
    k_pos = jnp.arange(past + T)
    sb = attend(lambda qa, qp: sb_block(qa[0], k_all, v_all, qp, k_pos), (q,), q_pos)
    mix = jnp.concatenate([s5_out, sb.reshape(bsz, T, SB_WIDTH).astype(h.dtype)], axis=-1)
    return mix, (k, v, h_last.real, h_last.imag)


def odd_mixer(h, p, i, lb, cache):
    f32 = jnp.float32
    bsz, T, _ = h.shape
    proj = h @ p['w_in_b'][i]
    cuts = np.cumsum([HG_WIDTH] * 4 + [FOX_WIDTH] * 3).tolist()
    q_h, f_h, i_h, g_h, fq, fk, fv, ff = jnp.split(proj, cuts, axis=-1)
    forget = lb + (1.0 - lb) * jax.nn.sigmoid(f_h.astype(f32))
    hq = jax.nn.silu(q_h.astype(f32)).reshape(bsz, T, HG_HEADS, HG_DK)
    hk = (1.0 - forget).reshape(bsz, T, HG_HEADS, HG_DK)
    hlog = jnp.log(forget).reshape(bsz, T, HG_HEADS, HG_DK)
    hv = i_h.astype(f32).reshape(bsz, T, HG_HEADS, HG_DV)
    if cache is None:
        S0 = jnp.zeros((bsz, HG_HEADS, HG_DK, HG_DV), f32)
    else:
        S0 = cache['hgrn'][i].astype(f32)
    o, S_last = hgrn2_recurrence(hq, hk, hv, hlog, S0)
    o = o * lax.rsqrt(jnp.mean(o * o, axis=-1, keepdims=True) + NORM_EPS)
    hg_out = o.reshape(bsz, T, HG_WIDTH) * p['hg_norm_g'][i].astype(f32) * jax.nn.silu(g_h.astype(f32))
    q = fq.reshape(bsz, T, FOX_HEADS, FOX_HEAD_DIM)
    k = fk.reshape(bsz, T, FOX_HEADS, FOX_HEAD_DIM)
    v = fv.reshape(bsz, T, FOX_HEADS, FOX_HEAD_DIM)
    log_f = jax.nn.log_sigmoid(ff.astype(f32) + p['fox_b_f'][i].astype(f32))
    if cache is None:
        k_all, v_all, lf_all, past = k, v, log_f, 0
    else:
        k_all = jnp.concatenate([cache['fox_k'][i], k], axis=1)
        v_all = jnp.concatenate([cache['fox_v'][i], v], axis=1)
        lf_all = jnp.concatenate([cache['fox_logf'][i].astype(f32), log_f], axis=1)
        past = cache['fox_k'].shape[2]
    c_all = jnp.cumsum(lf_all, axis=1)
    q_pos = past + jnp.arange(T)
    k_pos = jnp.arange(past + T)
    fox = attend(lambda qa, qp: fox_block(qa[0], qa[1], k_all, v_all, c_all, qp, k_pos), (q, c_all[:, past:]), q_pos)
    mix = jnp.concatenate([hg_out.astype(h.dtype), fox.reshape(bsz, T, FOX_WIDTH).astype(h.dtype)], axis=-1)
    return mix, (k, v, log_f, S_last)


def conv_ffn(h, w_up, conv_w, conv_b, w_down, prev):
    T = h.shape[1]
    up = h @ w_up
    a, g = up[..., :D_FF], up[..., D_FF:]
    a_ext = jnp.concatenate([prev.astype(a.dtype), a], axis=1)
    conv = sum(a_ext[:, j:j + T] * conv_w[j] for j in range(CONV_W)) + conv_b
    hid = jax.nn.silu(conv) * g
    return hid @ w_down, a_ext[:, T:]


def trunk(x, p, cache):
    f32 = jnp.float32
    bsz = x.shape[0]
    lb_soft = jax.nn.softmax(p['hg_lb_logits'].astype(f32), axis=0)
    lower_bounds = jnp.cumsum(lb_soft, axis=0) - lb_soft[0]
    new = {name: [] for name in STATE_KEYS}
    for l in range(DEPTH):
        i = l // 2
        h = rmsnorm(x, p['norm_mix_g'][l])
        if l % 2 == 0:
            mix, (k_new, v_new, h_re, h_im) = even_mixer(h, p, i, cache)
            new['sb_k'].append(k_new)
            new['sb_v'].append(v_new)
            new['s5_re'].append(h_re)
            new['s5_im'].append(h_im)
            x = x + mix @ p['w_out_a'][i]
        else:
            mix, (k_new, v_new, lf_new, s_new) = odd_mixer(h, p, i, lower_bounds[l], cache)
            new['fox_k'].append(k_new)
            new['fox_v'].append(v_new)
            new['fox_logf'].append(lf_new)
            new['hgrn'].append(s_new)
            x = x + mix @ p['w_out_b'][i]
        if cache is None:
            prev = jnp.zeros((bsz, CONV_W - 1, D_FF), x.dtype)
        else:
            prev = cache['conv'][l]
        ffn_out, conv_new = conv_ffn(rmsnorm(x, p['norm_ffn_g'][l]), p['ffn_w_up'][l], p['ffn_conv_w'][l],
                                     p['ffn_conv_b'][l], p['ffn_w_down'][l], prev)
        new['conv'].append(conv_new)
        x = x + ffn_out
    y = rmsnorm(x, p['final_norm_g'])
    return y, {name: jnp.stack(vals) for name, vals in new.items()}


def setup_inputs(seed: int = 0) -> dict:
    key = jax.random.key(seed)
    ks = iter(jax.random.split(key, 48))
    f32 = jnp.float32

    def nrm(shape, scale):
        return jax.random.normal(next(ks), shape, f32) * scale

    in_a = S5_WIDTH + 3 * SB_WIDTH
    in_b = 4 * HG_WIDTH + 3 * FOX_WIDTH + FOX_HEADS
    return {
        'x_prompt': nrm((BATCH, SEQ, D_MODEL), 1.0),
        'x_sample': nrm((DEC_BATCH, DEC_SEQ, D_MODEL), 1.0),
        'cache_sb_k': nrm((N_EVEN, DEC_BATCH, PAST_LEN, SB_HEADS, SB_HEAD_DIM), 1.0),
        'cache_sb_v': nrm((N_EVEN, DEC_BATCH, PAST_LEN, SB_HEADS, SB_HEAD_DIM), 1.0),
        'state_s5_re': nrm((N_EVEN, DEC_BATCH, S5_GROUPS, S5_STATE), 0.5),
        'state_s5_im': nrm((N_EVEN, DEC_BATCH, S5_GROUPS, S5_STATE), 0.5),
        'cache_fox_k': nrm((N_ODD, DEC_BATCH, PAST_LEN, FOX_HEADS, FOX_HEAD_DIM), 1.0),
        'cache_fox_v': nrm((N_ODD, DEC_BATCH, PAST_LEN, FOX_HEADS, FOX_HEAD_DIM), 1.0),
        'cache_fox_logf': jax.nn.log_sigmoid(3.0 + nrm((N_ODD, DEC_BATCH, PAST_LEN, FOX_HEADS), 1.0)),
        'state_hgrn': nrm((N_ODD, DEC_BATCH, HG_HEADS, HG_DK, HG_DV), 0.5),
        'state_ffn_conv': nrm((DEPTH, DEC_BATCH, CONV_W - 1, D_FF), 1.0),
        'norm_mix_g': 1.0 + nrm((DEPTH, D_MODEL), 0.02),
        'norm_ffn_g': 1.0 + nrm((DEPTH, D_MODEL), 0.02),
        'w_in_a': nrm((N_EVEN, D_MODEL, in_a), D_MODEL ** -0.5),
        's5_lambda_re': -0.5 + nrm((N_EVEN, S5_GROUPS, S5_STATE), 0.01),
        's5_lambda_im': math.pi * jnp.arange(S5_STATE, dtype=f32) + nrm((N_EVEN, S5_GROUPS, S5_STATE), 0.01),
        's5_log_dt': jax.random.uniform(next(ks), (N_EVEN, S5_GROUPS), f32, math.log(1e-3), math.log(1e-1)),
        's5_b_re': nrm((N_EVEN, S5_GROUPS, S5_STATE, S5_GROUP_CH), (2 * S5_GROUP_CH) ** -0.5),
        's5_b_im': nrm((N_EVEN, S5_GROUPS, S5_STATE, S5_GROUP_CH), (2 * S5_GROUP_CH) ** -0.5),
        's5_c_re': nrm((N_EVEN, S5_GROUPS, S5_GROUP_CH, S5_STATE), (2 * S5_STATE) ** -0.5),
        's5_c_im': nrm((N_EVEN, S5_GROUPS, S5_GROUP_CH, S5_STATE), (2 * S5_STATE) ** -0.5),
        's5_d': nrm((N_EVEN, S5_WIDTH), 1.0),
        's5_w_glu': nrm((N_EVEN, S5_WIDTH, S5_WIDTH), S5_WIDTH ** -0.5),
        's5_b_glu': nrm((N_EVEN, S5_WIDTH), 0.02),
        'w_out_a': nrm((N_EVEN, S5_WIDTH + SB_WIDTH, D_MODEL), (S5_WIDTH + SB_WIDTH) ** -0.5),
        'w_in_b': nrm((N_ODD, D_MODEL, in_b), D_MODEL ** -0.5),
        'hg_lb_logits': nrm((DEPTH, HG_WIDTH), 0.1),
        'hg_norm_g': 1.0 + nrm((N_ODD, HG_WIDTH), 0.02),
        'fox_b_f': 3.0 + nrm((N_ODD, FOX_HEADS), 0.5),
        'w_out_b': nrm((N_ODD, HG_WIDTH + FOX_WIDTH, D_MODEL), (HG_WIDTH + FOX_WIDTH) ** -0.5),
        'ffn_w_up': nrm((DEPTH, D_MODEL, 2 * D_FF), D_MODEL ** -0.5),
        'ffn_conv_w': nrm((DEPTH, CONV_W, D_FF), CONV_W ** -0.5),
        'ffn_conv_b': nrm((DEPTH, D_FF), 0.02),
        'ffn_w_down': nrm((DEPTH, D_FF, D_MODEL), D_FF ** -0.5),
        'final_norm_g': 1.0 + nrm((D_MODEL,), 0.02),
    }


def reference(x_prompt, x_sample, cache_sb_k, cache_sb_v, state_s5_re, state_s5_im, cache_fox_k, cache_fox_v,
              cache_fox_logf, state_hgrn, state_ffn_conv, norm_mix_g, norm_ffn_g, w_in_a, s5_lambda_re,
              s5_lambda_im, s5_log_dt, s5_b_re, s5_b_im, s5_c_re, s5_c_im, s5_d, s5_w_glu, s5_b_glu, w_out_a,
              w_in_b, hg_lb_logits, hg_norm_g, fox_b_f, w_out_b, ffn_w_up, ffn_conv_w, ffn_conv_b, ffn_w_down,
              final_norm_g):
    p = dict(norm_mix_g=norm_mix_g, norm_ffn_g=norm_ffn_g, w_in_a=w_in_a, s5_lambda_re=s5_lambda_re,
             s5_lambda_im=s5_lambda_im, s5_log_dt=s5_log_dt, s5_b_re=s5_b_re, s5_b_im=s5_b_im,
             s5_c_re=s5_c_re, s5_c_im=s5_c_im, s5_d=s5_d, s5_w_glu=s5_w_glu, s5_b_glu=s5_b_glu,
             w_out_a=w_out_a, w_in_b=w_in_b, hg_lb_logits=hg_lb_logits, hg_norm_g=hg_norm_g,
             fox_b_f=fox_b_f, w_out_b=w_out_b, ffn_w_up=ffn_w_up, ffn_conv_w=ffn_conv_w,
             ffn_conv_b=ffn_conv_b, ffn_w_down=ffn_w_down, final_norm_g=final_norm_g)
    cache = dict(sb_k=cache_sb_k, sb_v=cache_sb_v, s5_re=state_s5_re, s5_im=state_s5_im,
                 fox_k=cache_fox_k, fox_v=cache_fox_v, fox_logf=cache_fox_logf, hgrn=state_hgrn,
                 conv=state_ffn_conv)
    y_prompt, pn = trunk(x_prompt, p, None)
    y_sample, sn = trunk(x_sample, p, cache)
    return (y_prompt, y_sample,
            pn['sb_k'], pn['sb_v'], pn['s5_re'], pn['s5_im'], pn['fox_k'], pn['fox_v'], pn['fox_logf'],
            pn['hgrn'], pn['conv'],
            sn['sb_k'], sn['sb_v'], sn['s5_re'], sn['s5_im'], sn['fox_k'], sn['fox_v'], sn['fox_logf'],
            sn['hgrn'], sn['conv'])
```

```python
import contextlib
import os
SEM_ROT = int(os.environ.get("SEM_ROT", "30000"))
KSTAGE = os.environ.get('KSTAGE', 'all')
DBG_MIX = os.environ.get('DBG_MIX', '0') == '1'
import numpy as np
import concourse.bass as bass
import concourse.mybir as mybir
from concourse.bass_utils import run_bass_kernel_spmd

F32 = mybir.dt.float32
BF16 = mybir.dt.bfloat16
ALU = mybir.AluOpType
AF = mybir.ActivationFunctionType

D = 1024
KC = 8
NT = 512
EPS = 1e-6
T_FULL = 16384
P_FULL = 4096
DS = 16
NSB = 4


class Trk:
    def __init__(self, ap=None, name=""):
        self.ap = ap
        self.name = name
        self.w = None
        self.rs = {}
        self.dsem = None
        self.dcnt = 0

    def __getitem__(self, idx):
        return self.ap[idx]


class Sched:
    ENG = ("pe", "act", "dve", "pool", "sp")

    def __init__(self, nc):
        self.nc = nc
        self.eobj = {"pe": nc.tensor, "act": nc.scalar, "dve": nc.vector, "pool": nc.gpsimd, "sp": nc.sync}
        self.q = {e: [] for e in self.ENG}
        self.sem = {}
        self.cnt = {}
        self.waited = {e: {} for e in self.ENG}
        self.nsem = 0
        for e in self.ENG:
            self._new_sem(e)
        self.dma_tiles = []

    def _alloc(self, name):
        self.nsem += 1
        return self.nc.alloc_semaphore(name=f"{name}_{self.nsem}")

    def _new_sem(self, e):
        self.sem[e] = self._alloc("e" + e)
        self.cnt[e] = 0

    def _wait(self, eng, deps):
        for d in deps:
            if d is None:
                continue
            kind, a, b, src = d
            if kind == "c":
                sem, val = a, b
                if src == eng == "pe":
                    continue
            else:
                sem, val = a.dsem, a.dcnt
            key = id(sem)
            if self.waited[eng].get(key, 0) < val:
                self.q[eng].append(("w", sem, val))
                self.waited[eng][key] = val

    def _deps(self, reads, writes):
        deps = []
        for t in reads:
            deps.append(t.w)
        for t in writes:
            deps.append(t.w)
            deps.extend(t.rs.values())
        return deps

    def _mark(self, me, key, reads, writes):
        for t in reads:
            t.rs[key] = me
        for t in writes:
            t.w = me
            t.rs = {}

    def op(self, eng, fn, reads=(), writes=()):
        self._wait(eng, self._deps(reads, writes))
        if self.cnt[eng] >= SEM_ROT:
            self._new_sem(eng)
        self.cnt[eng] += 1
        sem = self.sem[eng]
        me = ("c", sem, self.cnt[eng], eng)
        self.q[eng].append(("i", fn, sem))
        self._mark(me, id(sem), reads, writes)

    def dma(self, eng, out, in_, dt, reads=(), writes=(), **kw):
        self._wait(eng, self._deps(reads, writes))
        if dt.dsem is None:
            dt.dsem = self._alloc("d")
            self.dma_tiles.append(dt)
        dt.dcnt += 16
        me = ("d", dt, None, eng)
        self.q[eng].append(("d", lambda e, o=out, i=in_, k=kw: e.dma_start(out=o, in_=i, **k), dt.dsem))
        self._mark(me, ("d", id(dt)), reads, writes)

    def raw(self, eng, fn, sem, inc, reads=(), writes=(), dt=None):
        self._wait(eng, self._deps(reads, writes))
        dt.dcnt += inc
        me = ("d", dt, None, eng)
        self.q[eng].append(("r", fn, sem, inc))
        self._mark(me, ("d", id(dt)), reads, writes)

    def barrier(self):
        for e in self.ENG:
            for o in self.ENG:
                if o != e and self.cnt[o] > 0:
                    key = id(self.sem[o])
                    if self.waited[e].get(key, 0) < self.cnt[o]:
                        self.q[e].append(("w", self.sem[o], self.cnt[o]))
                        self.waited[e][key] = self.cnt[o]
            for dt in self.dma_tiles:
                key = id(dt.dsem)
                if self.waited[e].get(key, 0) < dt.dcnt:
                    self.q[e].append(("w", dt.dsem, dt.dcnt))
                    self.waited[e][key] = dt.dcnt

    def wait_tiles(self, eng, tiles):
        for dt in tiles:
            if dt.dsem is None:
                continue
            key = id(dt.dsem)
            if self.waited[eng].get(key, 0) < dt.dcnt:
                self.q[eng].append(("w", dt.dsem, dt.dcnt))
                self.waited[eng][key] = dt.dcnt

    def finish(self):
        for dt in self.dma_tiles:
            self.q["sp"].append(("w", dt.dsem, dt.dcnt))
        for e in self.ENG:
            if e != "sp" and self.cnt[e] > 0:
                self.q["sp"].append(("w", self.sem[e], self.cnt[e]))

    def emit(self, block):
        def replay(eng):
            def body(e):
                for it in self.q[eng]:
                    if it[0] == "w":
                        e.wait_ge(it[1], it[2])
                    elif it[0] == "i":
                        it[1](e).then_inc(it[2], 1)
                    elif it[0] == "d":
                        it[1](e).then_inc(it[2], 16)
                    else:
                        it[1](e).then_inc(it[2], it[3])
            return body
        block.tensor(replay("pe"))
        block.scalar(replay("act"))
        block.vector(replay("dve"))
        block.gpsimd(replay("pool"))
        block.sync(replay("sp"))


class Opnd:
    def __init__(self, t, ap):
        self.t = t
        self.ap = ap


def _o(trk, idx=None):
    return Opnd(trk, trk.ap[idx] if idx is not None else trk.ap)


Trk.__call__ = lambda self, *idx: Opnd(self, self.ap[idx if len(idx) != 1 else idx[0]])


def _rw(*ops):
    return [o.t for o in ops if isinstance(o, Opnd)]


def _a(x):
    return x.ap if isinstance(x, Opnd) else x


class Ops:
    def __init__(self, S):
        self.S = S

    def tt(self, eng, out, a, b, op):
        self.S.op(eng, lambda e: e.tensor_tensor(out=out.ap, in0=a.ap, in1=b.ap, op=op), reads=_rw(a, b), writes=[out.t])

    def ts(self, eng, out, a, s1, op0, s2=None, op1=None):
        if op1 is None:
            self.S.op(eng, lambda e: e.tensor_scalar(out=out.ap, in0=a.ap, scalar1=_a(s1), scalar2=None, op0=op0),
                      reads=_rw(a, s1), writes=[out.t])
        else:
            self.S.op(eng, lambda e: e.tensor_scalar(out=out.ap, in0=a.ap, scalar1=_a(s1), scalar2=_a(s2), op0=op0, op1=op1),
                      reads=_rw(a, s1, s2), writes=[out.t])

    def stt(self, out, a, s, b, op0, op1):
        self.S.op("dve", lambda e: e.scalar_tensor_tensor(out=out.ap, in0=a.ap, scalar=_a(s), in1=b.ap, op0=op0, op1=op1),
                  reads=_rw(a, s, b), writes=[out.t])

    def act(self, out, a, func, scale=None, bias=None):
        kw = {}
        if scale is not None:
            kw["scale"] = _a(scale)
        if bias is not None:
            kw["bias"] = _a(bias)
        self.S.op("act", lambda e: e.activation(out=out.ap, in_=a.ap, func=func, **kw), reads=_rw(a, scale, bias), writes=[out.t])

    def copy(self, eng, out, a):
        self.S.op(eng, lambda e: e.tensor_copy(out=out.ap, in_=a.ap), reads=[a.t], writes=[out.t])

    def scan(self, out, d0, d1, init, op0=ALU.mult, op1=ALU.add):
        self.S.op("dve", lambda e: e.tensor_tensor_scan(out=out.ap, data0=d0.ap, data1=d1.ap, initial=_a(init), op0=op0, op1=op1),
                  reads=_rw(d0, d1, init), writes=[out.t])

    def mm(self, out, lhsT, rhs, start=True, stop=True, skip=False):
        self.S.op("pe", lambda e: e.matmul(out.ap, lhsT=lhsT.ap, rhs=rhs.ap, start=start, stop=stop, skip_group_check=skip),
                  reads=[lhsT.t, rhs.t] + ([] if start else [out.t]), writes=[out.t])

    def memset(self, eng, out, val):
        self.S.op(eng, lambda e: e.memset(out.ap, val), writes=[out.t])

    def ld(self, out, src, eng="sp"):
        self.S.dma(eng, out.ap, src, out.t, writes=[out.t])

    def st(self, dst, src, eng="pool"):
        self.S.dma(eng, dst, src.ap, src.t, reads=[src.t])

PI = float(np.pi)


class _Cut(Exception):
    pass


def cut(name):
    if KSTAGE == name:
        raise _Cut()


def build(T=T_FULL, P=P_FULL):
    nc = bass.Bass("TRN2", target_bir_lowering=False)
    S = Sched(nc)
    O = Ops(S)
    es = contextlib.ExitStack()
    NS = NSB * DS
    A_ = slice(None)

    def din(name, shape, dt=F32):
        return nc.dram_tensor(name, list(shape), dt, kind="ExternalInput").ap()

    def dout(name, shape, dt=F32):
        return nc.dram_tensor(name, list(shape), dt, kind="ExternalOutput").ap()

    scopes = [es]

    uid = [0]

    def sb(name, shape, dt=F32):
        uid[0] += 1
        t = scopes[-1].enter_context(nc.sbuf_tensor(f"{name}_u{uid[0]}", list(shape), dt))
        return Trk(t, name)

    def tmp_open():
        scopes.append(contextlib.ExitStack())

    def tmp_close():
        S.barrier()
        scopes.pop().close()

    def ps(name, shape, dt=F32):
        t = es.enter_context(nc.psum_tensor(name, list(shape), dt))
        return Trk(t, name)

    xT_p = din("xT_p", [D, T])
    xT_s = din("xT_s", [D, NS])
    g_mix0 = din("g_mix0", [128, KC])
    w_in_a = din("w_in_a", [D, 512])
    s5col = din("s5col", [128, 3, 4])
    s5row = din("s5row", [128, 3, 512])
    s5B = din("s5B", [128, 2, 512])
    s5C = din("s5C", [128, 2, 512])
    s5d = din("s5d", [128, 1])
    s5h0 = din("s5h0", [128, 2, NSB, 4])
    cmask = din("cmask", [128, 2, 896])
    ctri = din("ctri", [128, 4, 128])
    sbkc = din("sbkc", [128, NSB, P])
    sbvc = din("sbvc", [P, NSB, 128])
    TP = T + NS
    DFF = 2816
    NF = DFF // 128
    NFL = 6
    WOUT = [din(f"w_out{l}", [D, D]) for l in range(2)]
    w_glu = din("w_glu", [512, 512])
    b_glu = din("b_glu", [128, 4])
    GFF = [din(f"g_ffn{l}", [128, KC]) for l in range(2)]
    WUP = [din(f"w_up{l}", [D, NFL, 256]) for l in range(2)]
    WDN = [din(f"w_dn{l}", [NFL * 128, D]) for l in range(2)]
    CW = [din(f"cw{l}", [128, NFL, 4]) for l in range(2)]
    CST = [din(f"cst{l}", [128, NSB, NFL, 2]) for l in range(2)]
    g_fin = din("g_fin", [128, KC])
    g_mix1 = din("g_mix1", [128, KC])
    w_in_b = din("w_in_b", [D, 1024])
    hgp = din("hgp", [128, 3])
    bfrow = din("bfrow", [128, 8])
    hgS0 = din("hgS0", [128, NSB, 128])
    fkc = din("fkc", [128, NSB, P])
    fvc = din("fvc", [P, NSB, 128])
    flc = din("flc", [128, NSB, P // 128, 2])
    o_fkT = dout("o_fkT", [128, T])
    o_fv = dout("o_fv", [T, 128])
    o_flf = dout("o_flf", [T, 2])
    o_fkT_s = dout("o_fkT_s", [128, NS])
    o_fv_s = dout("o_fv_s", [NS, 128])
    o_flf_s = dout("o_flf_s", [NS, 2])
    o_hg = dout("o_hg", [128, (1 + NSB) * 128])
    o_y = dout("o_y", [D, TP])
    XS = nc.dram_tensor("XS", [D, TP], F32).ap()
    OCONV = [dout(f"o_conv{l}", [128, (1 + NSB) * NFL * 2]) for l in range(2)]
    NTL = T // NT + 1
    tile_n = [NT] * (T // NT) + [NS]
    mixg_t = Trk(None, "mixg")
    mixg_t.dsem = S._alloc("cc")
    S.dma_tiles.append(mixg_t)

    def mk_exch(tag):
        MB = [nc.dram_tensor(f"MIXB{tag}_{i}", [256, tile_n[i]], BF16).ap() for i in range(NTL)]
        MG = [nc.dram_tensor(f"MIXG{tag}_{i}", [1024, tile_n[i]], BF16).ap() for i in range(NTL)]
        return MB, MG

    def exchange(MBi, MGi, srcs):
        S.wait_tiles("pool", list(srcs))
        S.raw("pool", lambda e: e.collective_compute("AllGather", ALU.bypass, replica_groups=[[0, 1, 2, 3], [4, 5, 6, 7]],
                                                      ins=[MBi.opt()], outs=[MGi.opt()]),
              mixg_t.dsem, 1, dt=mixg_t)
        mixg_t.w = ("d", mixg_t, None, "pool")

    MB0, MG0 = mk_exch("a")
    o_kT = dout("o_kT", [128, T])
    o_v = dout("o_v", [T, 128])
    o_kT_s = dout("o_kT_s", [128, NS])
    o_v_s = dout("o_v_s", [NS, 128])
    o_s5 = dout("o_s5", [128, (1 + NSB) * 8])

    ones_bf = sb("ones_bf", [128, 128], BF16)
    O.memset("pool", ones_bf(), 1.0)
    onesf = sb("onesf", [128, NT])
    O.memset("pool", onesf(), 1.0)
    gm0 = sb("gm0", [128, KC])
    O.ld(gm0(), g_mix0[:, :])
    gfin = sb("gfin", [128, KC])
    O.ld(gfin(), g_fin[:, :])
    masks_bf = sb("masks_bf", [128, 2, 896], BF16)
    tri_bf = sb("tri_bf", [128, 4, 128], BF16)
    trif = sb("trif", [128, 128])
    bank = [ps(f"bank{i}", [128, NT]) for i in range(8)]
    tmp_open()
    wa = sb("wa", [128, KC, 512], BF16)
    LT = NT
    cosT = [sb(f"cosT{j}", [128, LT]) for j in range(4)]
    sinT = [sb(f"sinT{j}", [128, LT]) for j in range(4)]
    rhoT = [sb(f"rhoT{j}", [128, LT]) for j in range(4)]
    Bbre = sb("Bbre", [128, 512], BF16)
    Bbim = sb("Bbim", [128, 512], BF16)
    Cre = sb("Cre", [128, 512], BF16)
    nCim = sb("nCim", [128, 512], BF16)
    dcol = sb("s5dcol", [128, 1])
    z0 = sb("z0", [128, (1 + NSB) * 8])
    k_all = sb("k_all", [128, T], BF16)
    v_all = sb("v_all", [128, T // 128, 128], BF16)
    tmp_open()
    wst = sb("wst", [128, KC, 512])
    mst = sb("mst", [128, 2, 896])
    O.ld(mst(), cmask[:, :, :])
    O.copy("pool", masks_bf(), mst())
    tst = sb("tst", [128, 4, 128])
    O.ld(tst(), ctri[:, :, :])
    O.copy("pool", tri_bf(), tst())
    O.copy("pool", trif(), tst(A_, 2, A_))
    O.ld(wst(), w_in_a.rearrange("(kc p) n -> p kc n", p=128))
    for kc in range(KC):
        O.copy("pool" if kc % 2 else "dve", wa(slice(None), kc, slice(None)), wst(slice(None), kc, slice(None)))


    try:
        pc = sb("s5pc", [128, 3, 4])
        O.ld(pc(), s5col[:, :, :])
        pr = sb("s5pr", [128, 3, 512])
        O.ld(pr(), s5row[:, :, :])
        Bst = sb("s5Bst", [128, 2, 512])
        O.ld(Bst(), s5B[:, :, :])
        Cst = sb("s5Cst", [128, 2, 512])
        O.ld(Cst(), s5C[:, :, :])
        O.ld(dcol(), s5d[:, :])
        h0 = sb("s5h0t", [128, 2, NSB, 4])
        O.ld(h0(), s5h0[:, :, :, :])
        cut('c_loads')

        def trig(theta, n, name):
            k = sb(name + "_k", [128, n])
            tmp = sb(name + "_t", [128, n])
            red = sb(name + "_r", [128, n])
            O.ts("dve", k(), theta, PI, ALU.is_gt)
            for m in range(2, 8):
                O.ts("dve", tmp(), theta, (2 * m - 1) * PI, ALU.is_gt)
                O.tt("dve", k(), k(), tmp(), ALU.add)
            O.stt(red(), k(), -2 * PI, theta, ALU.mult, ALU.add)
            sn = sb(name + "_sin", [128, n])
            cs = sb(name + "_cos", [128, n])
            O.act(sn(), red(), AF.Sin)
            O.ts("dve", tmp(), red(), PI / 2, ALU.is_gt)
            O.ts("dve", tmp(), tmp(), -2 * PI, ALU.mult, PI / 2, ALU.add)
            O.tt("dve", tmp(), tmp(), red(), ALU.add)
            O.act(cs(), tmp(), AF.Sin)
            return cs, sn

        dtc = sb("dtc", [128, 4])
        O.act(dtc(), pc(A_, 2, A_), AF.Exp)
        rhoc = sb("rhoc", [128, 4])
        O.tt("dve", rhoc(), pc(A_, 0, A_), dtc(), ALU.mult)
        O.act(rhoc(), rhoc(), AF.Exp)
        thc = sb("thc", [128, 4])
        O.tt("dve", thc(), pc(A_, 1, A_), dtc(), ALU.mult)
        cosc, sinc = trig(thc(), 4, "tc")
        cut('c_trig')
        nsm = sb("nsm", [128, 1])
        for j in range(4):
            O.ts("dve", rhoT[j](), onesf(A_, slice(0, LT)), rhoc(A_, slice(j, j + 1)), ALU.mult)
            O.copy("dve", cosT[j](A_, slice(0, 1)), cosc(A_, slice(j, j + 1)))
            O.copy("dve", sinT[j](A_, slice(0, 1)), sinc(A_, slice(j, j + 1)))
            m = 1
            while m < LT:
                cm = cosT[j](A_, slice(m - 1, m))
                sm = sinT[j](A_, slice(m - 1, m))
                O.ts("dve", nsm(), sm, -1.0, ALU.mult)
                lo, hi = slice(0, m), slice(m, 2 * m)
                O.ts("dve", cosT[j](A_, hi), cosT[j](A_, lo), cm, ALU.mult)
                O.stt(cosT[j](A_, hi), sinT[j](A_, lo), nsm(), cosT[j](A_, hi), ALU.mult, ALU.add)
                O.ts("dve", sinT[j](A_, hi), cosT[j](A_, lo), sm, ALU.mult)
                O.stt(sinT[j](A_, hi), sinT[j](A_, lo), cm, sinT[j](A_, hi), ALU.mult, ALU.add)
                m *= 2

        cut('c_tab')
        lr, li = pr(A_, 0, A_), pr(A_, 1, A_)
        dtr = sb("dtr", [128, 512])
        O.act(dtr(), pr(A_, 2, A_), AF.Exp)
        rr = sb("rr", [128, 512])
        O.tt("dve", rr(), lr, dtr(), ALU.mult)
        O.act(rr(), rr(), AF.Exp)
        thr = sb("thr", [128, 512])
        O.tt("dve", thr(), li, dtr(), ALU.mult)
        cosr, sinr = trig(thr(), 512, "tr")
        nre = sb("nre", [128, 512])
        nim = sb("nim", [128, 512])
        O.tt("dve", nre(), rr(), cosr(), ALU.mult)
        O.ts("dve", nre(), nre(), -1.0, ALU.add)
        O.tt("dve", nim(), rr(), sinr(), ALU.mult)
        den = sb("den", [128, 512])
        w1 = sb("w1", [128, 512])
        w2 = sb("w2", [128, 512])
        O.tt("dve", den(), lr, lr, ALU.mult)
        O.tt("dve", w1(), li, li, ALU.mult)
        O.tt("dve", den(), den(), w1(), ALU.add)
        S.op("dve", lambda e: e.reciprocal(out=den[:, :], in_=den[:, :]), reads=[den], writes=[den])
        kre = sb("kre", [128, 512])
        kim = sb("kim", [128, 512])
        O.tt("dve", kre(), nre(), lr, ALU.mult)
        O.tt("dve", w1(), nim(), li, ALU.mult)
        O.tt("dve", kre(), kre(), w1(), ALU.add)
        O.tt("dve", kre(), kre(), den(), ALU.mult)
        O.tt("dve", kim(), nim(), lr, ALU.mult)
        O.tt("dve", w1(), nre(), li, ALU.mult)
        O.tt("dve", kim(), kim(), w1(), ALU.subtract)
        O.tt("dve", kim(), kim(), den(), ALU.mult)
        O.tt("dve", w1(), kre(), Bst(A_, 0, A_), ALU.mult)
        O.tt("dve", w2(), kim(), Bst(A_, 1, A_), ALU.mult)
        O.tt("dve", Bbre(), w1(), w2(), ALU.subtract)
        O.tt("dve", w1(), kre(), Bst(A_, 1, A_), ALU.mult)
        O.tt("dve", w2(), kim(), Bst(A_, 0, A_), ALU.mult)
        O.tt("dve", Bbim(), w1(), w2(), ALU.add)
        O.copy("dve", Cre(), Cst(A_, 0, A_))
        O.ts("dve", nCim(), Cst(A_, 1, A_), -1.0, ALU.mult)

        cut('c_kap')
        O.memset("dve", z0(A_, slice(0, 8)), 0.0)
        for sb_ in range(NSB):
            for ri in range(2):
                O.copy("dve", z0(A_, slice(8 * (1 + sb_) + 4 * ri, 8 * (1 + sb_) + 4 * ri + 4)), h0(A_, ri, sb_, A_))

        cut('c_z0')
        tmp_close()
        cut('c_t0')
        NB = 2
        xt = [sb(f"xt{i}", [128, KC, NT]) for i in range(1)]
        sq = [sb(f"sq{i}", [128, KC, NT], BF16) for i in range(1)]
        hT = [sb(f"hT{i}", [128, KC, NT], BF16) for i in range(1)]
        lnv = [sb(f"lnv{i}", [128, NT]) for i in range(1)]
        rstd = [sb(f"rstd{i}", [128, NT]) for i in range(1)]
        kst = [sb(f"kst{i}", [128, NT]) for i in range(NB)]
        vst = [sb(f"vst{i}", [128, 4, 128]) for i in range(NB)]
        uf = [sb(f"uf{i}", [128, NT]) for i in range(1)]
        ub = [sb(f"ub{i}", [128, NT], BF16) for i in range(1)]
        s5o = [sb(f"s5o{i}", [128, NT], BF16) for i in range(NB)]
        tmpf = [sb(f"tmpf{i}", [128, NT]) for i in range(4)]
        vre, vim, zre, zim = (sb(n, [128, NT]) for n in ("vre", "vim", "zre", "zim"))
        hre = sb("hre", [128, NT], BF16)
        him = sb("him", [128, NT], BF16)
        ysum = vre
        ctmp = sb("ctmp", [128, 4])
        qT = sb("qT", [128, NT], BF16)
        e_ = [[sb(f"e_{h}{p}", [128, NT], BF16) for p in range(2)] for h in range(2)]
        sp_ = [[sb(f"sp_{h}{p}", [128, NT], BF16) for p in range(3)] for h in range(2)]
        x_ = [sb(f"x_{h}", [128, NT], BF16) for h in range(2)]
        w_ = [[sb(f"w_{h}{p}", [128, NT], BF16) for p in range(2)] for h in range(2)]
        mixsb = [[sb(f"mixsb{h}{p}", [64, NT], BF16) for p in range(1)] for h in range(2)]
        knew = [sb(f"knew{i}", [128, 128], BF16) for i in range(NSB)]
        vnew = [sb(f"vnew{i}", [128, 128], BF16) for i in range(NSB)]
        kcs = [sb(f"kcs{i}", [128, 512]) for i in range(1)] * 2
        kcb = [sb(f"kcb{i}", [128, 512], BF16) for i in range(2)]
        vcs = [sb(f"vcs{i}", [128, 4, 128]) for i in range(1)] * 2
        vcb = [sb(f"vcb{i}", [128, 4, 128], BF16) for i in range(2)]
        zb_double, Pb, Ob = [[bank[1], bank[7]], [bank[2], bank[0]]], [bank[3], bank[4]], [bank[5], bank[6]]
        zb_single = [[bank[1], bank[1]], [bank[2], bank[2]]]
        triI = tri_bf(A_, 0, A_)
        triC = tri_bf(A_, 1, A_)

        def sb_attend(qfn, n, blocks, outs, oc0, side=None):
            ns = slice(0, n)
            nb = len(blocks)
            info = {}
            zb = zb_double if side is None else zb_single
            pulls = 1 if nb >= 48 else -(-48 // nb)

            def stA(i):
                info[i] = blocks[i]()
                kf = info[i][0]
                for hd in range(2):
                    O.mm(zb[hd][i % 2](A_, ns), kf(hd), qfn(hd))

            def stB(i):
                mk = info[i][2]
                for hd in range(2):
                    E = e_[hd][i % 2]
                    O.act(E(A_, ns), zb[hd][i % 2](A_, ns), AF.Exp, scale=0.125)
                    if mk is not None:
                        O.tt("pool", E(A_, ns), E(A_, ns), mk, ALU.mult)
                for hd in range(2):
                    O.act(sp_[hd][i % 3](A_, ns), e_[hd][i % 2](A_, ns), AF.Ln, bias=1.0)

            def stC(i):
                for hd in range(2):
                    if i > 0:
                        O.mm(Pb[hd](A_, ns), triC, sp_[hd][(i - 1) % 3](A_, ns), start=False, stop=False, skip=True)
                    O.mm(Pb[hd](A_, ns), triI, sp_[hd][i % 3](A_, ns), start=(i == 0), stop=True, skip=True)

            def stD(i):
                for hd in range(2):
                    O.act(x_[hd](A_, ns), Pb[hd](A_, ns), AF.Exp, scale=-1.0)
                for hd in range(2):
                    O.tt("dve", w_[hd][i % 2](A_, ns), e_[hd][i % 2](A_, ns), x_[hd](A_, ns), ALU.mult)

            def stE(i):
                vf = info[i][1]
                for hd in range(2):
                    O.mm(Ob[hd](slice(0, 64), ns), vf(hd), w_[hd][i % 2](A_, ns), start=(i == 0), stop=(i == nb - 1))

            stA(0)
            stB(0)
            if nb > 1:
                stA(1)
            for i in range(nb):
                stC(i)
                if i + 1 < nb:
                    stB(i + 1)
                stD(i)
                if i + 2 < nb:
                    stA(i + 2)
                stE(i)
                if side is not None:
                    for _ in range(pulls):
                        next(side, None)
            if side is not None:
                for _ in side:
                    pass
            for hd in range(2):
                O.act(outs[hd](A_, slice(oc0, oc0 + n)), Ob[hd](slice(0, 64), ns), AF.Copy)

        def rms_tile(src_ap, n, gcol, it):
            b = it % NB
            X, Q, H, L, R = xt[0], sq[0], hT[0], lnv[0], rstd[0]
            S.dma("sp", X[:, :, :n], src_ap.rearrange("(kc p) n -> p kc n", p=128), X, writes=[X])
            O.act(Q(A_, A_, slice(0, n)), X(A_, A_, slice(0, n)), AF.Square)
            ssb = bank[0]
            for kc in range(KC):
                O.mm(ssb(A_, slice(0, n)), ones_bf(), Q(A_, kc, slice(0, n)), start=(kc == 0), stop=(kc == KC - 1))
            O.act(L(A_, slice(0, n)), ssb(A_, slice(0, n)), AF.Ln, scale=1.0 / D, bias=EPS)
            O.act(R(A_, slice(0, n)), L(A_, slice(0, n)), AF.Exp, scale=-0.5)
            for kc in range(KC):
                O.stt(H(A_, kc, slice(0, n)), X(A_, kc, slice(0, n)), gcol(A_, slice(kc, kc + 1)), R(A_, slice(0, n)),
                      ALU.mult, ALU.mult)
            return X, H

        def proj_fm(H, n, wt, c0, pbank, m=128):
            for kc in range(KC):
                O.mm(pbank(slice(0, m), slice(0, n)), wt(A_, kc, slice(c0, c0 + m)), H(A_, kc, slice(0, n)),
                     start=(kc == 0), stop=(kc == KC - 1))

        def proj_tm(H, n, wt, c0, pbank):
            nsub = (n + 127) // 128
            for s_ in range(nsub):
                m = min(128, n - s_ * 128)
                for kc in range(KC):
                    O.mm(pbank(slice(0, m), slice(s_ * 128, (s_ + 1) * 128)), H(A_, kc, slice(s_ * 128, s_ * 128 + m)),
                         wt(A_, kc, slice(c0, c0 + 128)), start=(kc == 0), stop=(kc == KC - 1))

        def s5_gen(UF, UB, c0, L, seq, OUT, bu, yb):
            cs = slice(c0, c0 + L)
            ls = slice(0, L)
            for j in range(4):
                js = slice(128 * j, 128 * j + 128)
                c_, s_ = cosT[j](A_, ls), sinT[j](A_, ls)
                O.mm(bu(A_, ls), Bbre(A_, js), UB(A_, cs))
                O.tt("dve", tmpf[0](A_, ls), c_, bu(A_, ls), ALU.mult)
                O.tt("dve", tmpf[3](A_, ls), s_, bu(A_, ls), ALU.mult)
                yield
                O.mm(bu(A_, ls), Bbim(A_, js), UB(A_, cs))
                O.tt("dve", tmpf[1](A_, ls), s_, bu(A_, ls), ALU.mult)
                O.tt("dve", tmpf[2](A_, ls), c_, bu(A_, ls), ALU.mult)
                yield
                O.tt("pool", vre(A_, ls), tmpf[0](A_, ls), tmpf[1](A_, ls), ALU.add)
                O.tt("pool", vim(A_, ls), tmpf[2](A_, ls), tmpf[3](A_, ls), ALU.subtract)
                yield
                O.scan(zre(A_, ls), rhoT[j](A_, ls), vre(A_, ls), z0(A_, slice(8 * seq + j, 8 * seq + j + 1)))
                yield
                O.scan(zim(A_, ls), rhoT[j](A_, ls), vim(A_, ls), z0(A_, slice(8 * seq + 4 + j, 8 * seq + 4 + j + 1)))
                yield
                O.tt("pool", tmpf[0](A_, ls), c_, zre(A_, ls), ALU.mult)
                O.tt("pool", tmpf[1](A_, ls), s_, zim(A_, ls), ALU.mult)
                yield
                O.tt("pool", hre(A_, ls), tmpf[0](A_, ls), tmpf[1](A_, ls), ALU.subtract)
                O.tt("pool", tmpf[2](A_, ls), s_, zre(A_, ls), ALU.mult)
                yield
                O.tt("pool", tmpf[3](A_, ls), c_, zim(A_, ls), ALU.mult)
                O.tt("pool", him(A_, ls), tmpf[2](A_, ls), tmpf[3](A_, ls), ALU.add)
                yield
                e_ = slice(L - 1, L)
                O.tt("dve", z0(A_, slice(8 * seq + j, 8 * seq + j + 1)), tmpf[0](A_, e_), tmpf[1](A_, e_), ALU.subtract)
                O.tt("dve", z0(A_, slice(8 * seq + 4 + j, 8 * seq + 4 + j + 1)), tmpf[2](A_, e_), tmpf[3](A_, e_), ALU.add)
                O.mm(yb(A_, ls), Cre(A_, js), hre(A_, ls), start=(j == 0), stop=False, skip=True)
                O.mm(yb(A_, ls), nCim(A_, js), him(A_, ls), start=False, stop=(j == 3), skip=True)
                yield
            O.stt(ysum(A_, ls), UF(A_, cs), dcol(A_, slice(0, 1)), yb(A_, ls), ALU.mult, ALU.add)
            O.tt("pool", tmpf[0](A_, ls), ysum(A_, ls), ysum(A_, ls), ALU.mult)
            yield
            O.ts("pool", tmpf[0](A_, ls), tmpf[0](A_, ls), 0.044715, ALU.mult, 1.0, ALU.add)
            O.tt("pool", tmpf[0](A_, ls), tmpf[0](A_, ls), ysum(A_, ls), ALU.mult)
            yield
            O.act(tmpf[1](A_, ls), tmpf[0](A_, ls), AF.Sigmoid, scale=1.5957691216057308)
            O.tt("dve", OUT(A_, cs), ysum(A_, ls), tmpf[1](A_, ls), ALU.mult)

        def s5_chunk(UF, UB, c0, L, seq, OUT):
            for _ in s5_gen(UF, UB, c0, L, seq, OUT, bank[5], bank[7]):
                pass

        def layer0_tile(src_ap, n, it, okT_ap, ov_ap, seqs, mc0, prompt_j):
            b = it % NB
            X, H = rms_tile(src_ap, n, gm0, it)
            kb = bank[1 + (it % 2)]
            proj_fm(H, n, wa, 256, kb)
            K = kst[b]
            O.act(K(A_, slice(0, n)), kb(A_, slice(0, n)), AF.Copy)
            O.st(okT_ap, K(A_, slice(0, n)))
            vb = bank[3]
            proj_tm(H, n, wa, 384, vb)
            V = vst[b]
            nsub = (n + 127) // 128
            m = min(128, n)
            S.op("dve", lambda e: e.tensor_copy(out=V[:m, :nsub, :], in_=vb[:m, :nsub * 128].rearrange("p (s c) -> p s c", c=128)),
                 reads=[vb], writes=[V])
            if n >= 128:
                S.dma("pool", ov_ap.rearrange("(s p) c -> p s c", p=128), V[:, :nsub, :], V, reads=[V])
            else:
                S.dma("pool", ov_ap, V[:n, 0, :], V, reads=[V])
            if KSTAGE == 'nou':
                return
            ubk = bank[4]
            proj_fm(H, n, wa, 0, ubk)
            O.act(uf[0](A_, slice(0, n)), ubk(A_, slice(0, n)), AF.Copy)
            O.copy("dve", ub[0](A_, slice(0, n)), uf[0](A_, slice(0, n)))
            MBi = MB0[it]
            side = None
            if prompt_j is not None:
                side = s5_gen(uf[0], ub[0], 0, NT, 0, s5o[b], bank[7], bank[0])
            else:
                for (c0, L, seq) in seqs:
                    s5_chunk(uf[0], ub[0], c0, L, seq, s5o[b])
                S.dma("pool", MBi[0:128, 0:n], s5o[b][:, :n], s5o[b], reads=[s5o[b]])
            cut('c_s5only')
            proj_fm(H, n, wa, 128, bank[4])
            O.act(qT(A_, slice(0, n)), bank[4](A_, slice(0, n)), AF.Copy)
            ms = [mixsb[0][0], mixsb[1][0]]
            if prompt_j is not None:
                j = prompt_j
                O.copy("pool", k_all(A_, slice(j * NT, j * NT + n)), K(A_, slice(0, n)))
                O.copy("pool", v_all(A_, slice(4 * j, 4 * j + 4), A_), V(A_, slice(0, 4), A_))
                blocks = []
                for kb_ in range(4 * j + 3, -1, -1):
                    m_ = kb_ - 4 * j
                    mk = masks_bf(A_, 0, slice(384 - 128 * m_, 896 - 128 * m_)) if m_ >= 0 else None
                    blocks.append(lambda kb_=kb_, mk=mk: (
                        lambda hd: k_all(slice(64 * hd, 64 * hd + 64), slice(kb_ * 128, kb_ * 128 + 128)),
                        lambda hd: v_all(A_, kb_, slice(64 * hd, 64 * hd + 64)), mk))
                sb_attend(lambda hd: qT(slice(64 * hd, 64 * hd + 64), slice(0, n)), n, blocks, ms, 0, side=side)
                S.dma("pool", MBi[0:128, 0:n], s5o[b][:, :n], s5o[b], reads=[s5o[b]])
            else:
                for sb_ in range(NSB):
                    c0 = DS * sb_
                    O.memset("pool", knew[sb_](), 0.0)
                    O.memset("pool", vnew[sb_](), 0.0)
                    O.copy("pool", knew[sb_](A_, slice(0, DS)), K(A_, slice(c0, c0 + DS)))
                    for kc in range(KC):
                        O.mm(bank[7](slice(0, DS), slice(0, 128)), H(A_, kc, slice(c0, c0 + DS)), wa(A_, kc, slice(384, 512)),
                             start=(kc == 0), stop=(kc == KC - 1))
                    O.copy("dve", vnew[sb_](slice(0, DS), A_), bank[7](slice(0, DS), slice(0, 128)))
                    blocks = [lambda sb_=sb_: (lambda hd: knew[sb_](slice(64 * hd, 64 * hd + 64), A_),
                                               lambda hd: vnew[sb_](A_, slice(64 * hd, 64 * hd + 64)),
                                               masks_bf(A_, 0, slice(384, 384 + DS)))]
                    for kb_ in range(P // 128 - 1, -1, -1):
                        def blk(kb_=kb_, sb_=sb_):
                            ci, sub = kb_ // 4, kb_ % 4
                            pr_ = ci % 2
                            if sub == 3:
                                O.ld(kcs[pr_](), sbkc[:, sb_, ci * 512:(ci + 1) * 512])
                                O.copy("pool", kcb[pr_](), kcs[pr_]())
                                O.ld(vcs[pr_](), sbvc[ci * 512:(ci + 1) * 512, sb_, :].rearrange("(s p) c -> p s c", p=128))
                                O.copy("pool", vcb[pr_](), vcs[pr_]())
                            return (lambda hd: kcb[pr_](slice(64 * hd, 64 * hd + 64), slice(sub * 128, sub * 128 + 128)),
                                    lambda hd: vcb[pr_](A_, sub, slice(64 * hd, 64 * hd + 64)), None)
                        blocks.append(blk)
                    sb_attend(lambda hd, c0=c0: qT(slice(64 * hd, 64 * hd + 64), slice(c0, c0 + DS)), DS, blocks, ms, c0)
            for hd in range(2):
                S.dma("pool", MBi[128 + 64 * hd:192 + 64 * hd, 0:n], ms[hd][:, :n], ms[hd], reads=[ms[hd]])
            exchange(MBi, MG0[it], [s5o[b], ms[0], ms[1]])

        it = 0
        for j in range(T // NT):
            layer0_tile(xT_p[:, j * NT:(j + 1) * NT], NT, it, o_kT[:, j * NT:(j + 1) * NT], o_v[j * NT:(j + 1) * NT, :],
                        [(0, NT, 0)], j * NT, j)
            it += 1
            cut('c_t1')
        layer0_tile(xT_s[:, :], NS, it, o_kT_s[:, :], o_v_s[:, :], [(DS * i, DS, 1 + i) for i in range(NSB)], T, None)
        it += 1
        O.st(o_s5[:, :], z0(), eng='sp')
        cut('nocc')

        def run_phaseC(LY, MGL):
            tmp_open()
            woutb = sb("woutb", [128, KC, D], BF16)
            wglub = sb("wglub", [128, 4, 512], BF16)
            wdb = sb("wdb", [128, NFL, D], BF16)
            bgl = sb("bgl", [128, 4])
            gf0 = sb("gf0", [128, KC])
            cwt = sb("cwt", [128, NFL, 4])
            acar = sb("acar", [128, (1 + NSB) * NFL * 2])
            stg = sb("stg", [128, KC, 512])
            O.ld(bgl(), b_glu[:, :])
            O.ld(gf0(), GFF[LY][:, :])
            O.ld(cwt(), CW[LY][:, :, :])
            O.memset("dve", acar(A_, slice(0, NFL * 2)), 0.0)
            S.dma("sp", acar[:, NFL * 2:], CST[LY].rearrange("p s c k -> p (s c k)"), acar, writes=[acar])
            for hh in range(2):
                O.ld(stg(), WOUT[LY][:, hh * 512:(hh + 1) * 512].rearrange("(kc p) n -> p kc n", p=128))
                O.copy("pool" if hh else "dve", woutb(A_, A_, slice(hh * 512, hh * 512 + 512)), stg())
            if LY == 0:
                O.ld(stg(A_, slice(0, 4), A_), w_glu.rearrange("(kc p) n -> p kc n", p=128))
                O.copy("dve", wglub(), stg(A_, slice(0, 4), A_))
            for c in range(NFL):
                O.ld(stg(A_, slice(0, 2), A_), WDN[LY][c * 128:(c + 1) * 128, :].rearrange("p (a n) -> p a n", a=2))
                S.op("pool" if c % 2 else "dve",
                     lambda e, c=c: e.tensor_copy(out=wdb[:, c, :].rearrange("p (a n) -> p a n", a=2), in_=stg[:, 0:2, :]),
                     reads=[stg], writes=[wdb])
            mixt = sb("mixt", [128, KC, NT], BF16)
            Xb = [sb(f"Xc{i}", [128, KC, NT]) for i in range(2)]
            art = sb("art", [128, KC, NT])
            ARI = [nc.dram_tensor(f"ARI{LY}_{i}", [D, tile_n[i]], F32).ap() for i in range(NTL)]
            ARO = [nc.dram_tensor(f"ARO{LY}_{i}", [D, tile_n[i]], F32).ap() for i in range(NTL)]
            ar_t = Trk(None, "ar")
            ar_t.dsem = S._alloc("ar")
            S.dma_tiles.append(ar_t)
            ar_t.w = ("d", ar_t, None, "pool")
            pend = []
            Q = sb("Qc", [128, KC, NT], BF16)
            H = sb("Hc", [128, KC, NT], BF16)
            Lc = sb("Lc", [128, NT])
            Rc = sb("Rc", [128, NT])
            yg = sb("yg", [128, 4, NT], BF16)
            sig = sb("sig", [128, NT])
            aext = [sb(f"aext{i}", [128, NT + 2]) for i in range(2)]
            ct = [sb(f"ct{i}", [128, NT]) for i in range(2)]
            sg = [sb(f"sg{i}", [128, NT]) for i in range(2)]
            hid = sb("hid", [128, NFL, NT], BF16)
            wus = [sb(f"wus{i}", [128, KC, 256]) for i in range(2)]
            wub = [sb(f"wub{i}", [128, KC, 256], BF16) for i in range(3)]
            WUPB = nc.dram_tensor(f"WUPB{LY}", [NFL, 128, KC * 256], BF16).ap()
            wupb_t = Trk(None, "wupb")
            for c in range(NFL):
                p_ = c % 2
                O.ld(wus[p_](), WUP[LY][:, c, :].rearrange("(kc p) n -> p kc n", p=128))
                O.copy("pool" if c % 2 else "dve", wub[p_](), wus[p_]())
                S.dma("pool", WUPB[c].rearrange("p (kc n) -> p kc n", kc=KC), wub[p_][:, :, :], wupb_t,
                      reads=[wub[p_]], writes=[wupb_t])
            xo = [sb(f"xo{i}", [128, NT]) for i in range(2)]

            def phaseC_tile(src_ap, n, mc0, segs, MGsrc, ti):
                ns = slice(0, n)
                X = Xb[ti % 2]
                S.dma("sp", mixt[:, :, :n], MGsrc.rearrange("(kc p) n -> p kc n", p=128), mixt,
                      reads=[mixg_t], writes=[mixt])
                S.dma("sp", X[:, :, :n], src_ap.rearrange("(kc p) n -> p kc n", p=128), X, writes=[X])
                for oc in range(4 if LY == 0 else 0):
                    gb = bank[1 + oc % 2]
                    for ic in range(4):
                        O.mm(gb(A_, ns), wglub(A_, ic, slice(oc * 128, oc * 128 + 128)), mixt(A_, 2 * ic, ns),
                             start=(ic == 0), stop=(ic == 3))
                    O.act(sig(A_, ns), gb(A_, ns), AF.Sigmoid, bias=bgl(A_, slice(oc, oc + 1)))
                    O.tt("dve", yg(A_, oc, ns), mixt(A_, 2 * oc, ns), sig(A_, ns), ALU.mult)
                for oc in range(KC):
                    ob = bank[3 + oc % 2]
                    for kc in range(KC):
                        rhs = yg(A_, kc // 2, ns) if (kc % 2 == 0 and LY == 0) else mixt(A_, kc, ns)
                        O.mm(ob(A_, ns), woutb(A_, kc, slice(oc * 128, oc * 128 + 128)), rhs, start=(kc == 0), stop=(kc == KC - 1))
                    O.tt("dve", X(A_, oc, ns), ob(A_, ns), X(A_, oc, ns), ALU.add)
                O.act(Q(A_, A_, ns), X(A_, A_, ns), AF.Square)
                for kc in range(KC):
                    O.mm(bank[0](A_, ns), ones_bf(), Q(A_, kc, ns), start=(kc == 0), stop=(kc == KC - 1))
                O.act(Lc(A_, ns), bank[0](A_, ns), AF.Ln, scale=1.0 / D, bias=EPS)
                O.act(Rc(A_, ns), Lc(A_, ns), AF.Exp, scale=-0.5)
                for kc in range(KC):
                    O.stt(H(A_, kc, ns), X(A_, kc, ns), gf0(A_, slice(kc, kc + 1)), Rc(A_, ns), ALU.mult, ALU.mult)
                for c in range(NFL):
                    p_ = c % 2
                    w3 = c % 3
                    S.dma("sp", wub[w3][:, :, :], WUPB[c].rearrange("p (kc n) -> p kc n", kc=KC), wub[w3],
                          reads=[wupb_t], writes=[wub[w3]])
                    ab = bank[1 + p_]
                    for kc in range(KC):
                        O.mm(ab(A_, ns), wub[w3](A_, kc, slice(0, 128)), H(A_, kc, ns), start=(kc == 0), stop=(kc == KC - 1))
                    gbk = bank[5 + p_]
                    for kc in range(KC):
                        O.mm(gbk(A_, ns), wub[w3](A_, kc, slice(128, 256)), H(A_, kc, ns), start=(kc == 0), stop=(kc == KC - 1))
                    AE, CT, SG = aext[p_], ct[p_], sg[p_]
                    for (c0, L, seq) in segs:
                        cb = (seq * NFL + c) * 2
                        O.copy("pool", AE(A_, slice(0, 2)), acar(A_, slice(cb, cb + 2)))
                        O.act(AE(A_, slice(2, 2 + L)), ab(A_, slice(c0, c0 + L)), AF.Copy)
                        O.ts("dve", CT(A_, slice(0, L)), AE(A_, slice(2, 2 + L)), cwt(A_, c, slice(2, 3)), ALU.mult,
                             cwt(A_, c, slice(3, 4)), ALU.add)
                        O.stt(CT(A_, slice(0, L)), AE(A_, slice(1, 1 + L)), cwt(A_, c, slice(1, 2)), CT(A_, slice(0, L)), ALU.mult, ALU.add)
                        O.stt(CT(A_, slice(0, L)), AE(A_, slice(0, L)), cwt(A_, c, slice(0, 1)), CT(A_, slice(0, L)), ALU.mult, ALU.add)
                        O.copy("pool", acar(A_, slice(cb, cb + 2)), AE(A_, slice(L, L + 2)))
                        O.act(SG(A_, slice(0, L)), CT(A_, slice(0, L)), AF.Silu)
                        O.tt("dve", hid(A_, c, slice(c0, c0 + L)), SG(A_, slice(0, L)), gbk(A_, slice(c0, c0 + L)), ALU.mult)
                for oc in range(KC):
                    ob = bank[3 + oc % 2]
                    for c in range(NFL):
                        O.mm(ob(A_, ns), wdb(A_, c, slice(oc * 128, oc * 128 + 128)), hid(A_, c, ns), start=(c == 0), stop=(c == NFL - 1))
                    XO = xo[oc % 2]
                    O.act(XO(A_, ns), ob(A_, ns), AF.Copy)
                    S.dma("pool", ARI[ti][oc * 128:(oc + 1) * 128, 0:n], XO[:, :n], XO, reads=[XO])
                if pend:
                    epilogue(*pend.pop())
                S.wait_tiles("pool", [xo[0], xo[1]])
                S.raw("pool", lambda e: e.collective_compute("AllReduce", ALU.add, replica_groups=[[0, 1, 2, 3], [4, 5, 6, 7]],
                                                              ins=[ARI[ti].opt()], outs=[ARO[ti].opt()]),
                      ar_t.dsem, 1, dt=ar_t)
                ar_t.w = ("d", ar_t, None, "pool")
                pend.append((ti, n, mc0))

            def epilogue(ti, n, mc0):
                ns = slice(0, n)
                X = Xb[ti % 2]
                S.dma("sp", art[:, :, :n], ARO[ti].rearrange("(kc p) n -> p kc n", p=128), art, reads=[ar_t], writes=[art])
                O.tt("dve", X(A_, A_, ns), X(A_, A_, ns), art(A_, A_, ns), ALU.add)
                if LY == 0:
                    S.dma("pool", XS[:, mc0:mc0 + n].rearrange("(kc p) n -> p kc n", p=128), X[:, :, :n], X, reads=[X])
                else:
                    O.act(Q(A_, A_, ns), X(A_, A_, ns), AF.Square)
                    for kc in range(KC):
                        O.mm(bank[0](A_, ns), ones_bf(), Q(A_, kc, ns), start=(kc == 0), stop=(kc == KC - 1))
                    O.act(Lc(A_, ns), bank[0](A_, ns), AF.Ln, scale=1.0 / D, bias=EPS)
                    O.act(Rc(A_, ns), Lc(A_, ns), AF.Exp, scale=-0.5)
                    for kc in range(KC):
                        O.stt(art(A_, kc, ns), X(A_, kc, ns), gfin(A_, slice(kc, kc + 1)), Rc(A_, ns), ALU.mult, ALU.mult)
                    S.dma("pool", o_y[:, mc0:mc0 + n].rearrange("(kc p) n -> p kc n", p=128), art[:, :, :n], art, reads=[art])

            for j in range(T // NT):
                phaseC_tile((xT_p if LY == 0 else XS)[:, j * NT:(j + 1) * NT], NT, j * NT, [(0, NT, 0)], MGL[j], j)
            phaseC_tile(xT_s[:, :] if LY == 0 else XS[:, T:T + NS], NS, T, [(DS * i, DS, 1 + i) for i in range(NSB)], MGL[T // NT], T // NT)
            epilogue(*pend.pop())
            O.st(OCONV[LY][:, :], acar(), eng='sp')


            tmp_close()

        tmp_close()
        run_phaseC(0, MG0)
        if DBG_MIX:
            S.barrier()
            dbgx = dout("o_xs", [D, TP])
            dbx_t = Trk(None, "dbx")
            S.dma("pool", dbgx[:, :], XS[:, :], dbx_t)
        cut('l0only')
        S.barrier()

        MB1, MG1 = mk_exch("b")
        tmp_open()
        NKB = T // 128
        NCB = P // 128
        wb = sb("wb", [128, KC, 1024], BF16)
        gm1 = sb("gm1", [128, KC])
        O.ld(gm1(), g_mix1[:, :])
        k1_all = sb("k1_all", [128, T], BF16)
        v1_all = sb("v1_all", [128, NKB, 2, 65], BF16)
        O.memset("pool", v1_all(), 1.0)
        cKM = sb("cKM", [128, NKB, 2])
        cKMs = sb("cKMs", [128, NCB + 1, 2])
        btab = sb("btab", [128, max(NKB, NCB + 1), 2])
        carry = sb("carry", [128, 2])
        shq = sb("shq", [128, 2])
        O.memset("dve", carry(), 0.0)
        hgpt = sb("hgpt", [128, 3])
        O.ld(hgpt(), hgp[:, :])
        bfr = sb("bfr", [128, 8])
        O.ld(bfr(), bfrow[:, :])
        Sst = sb("Sst", [128, 1 + NSB, 128])
        O.memset("dve", Sst(A_, 0, A_), 0.0)
        S.dma("sp", Sst[:, 1:, :], hgS0[:, :, :], Sst, writes=[Sst])
        Sbf = sb("Sbf", [128, 128], BF16)
        lbe = sb("lbe", [128, 2])
        lbcol = sb("lbcol", [128, 1])
        omlb = sb("omlb", [128, 1])
        O.act(lbe(), hgpt(A_, slice(0, 2)), AF.Exp)
        O.tt("dve", lbcol(), lbe(A_, slice(0, 1)), lbe(A_, slice(1, 2)), ALU.add)
        S.op("dve", lambda e: e.reciprocal(out=lbcol[:, :], in_=lbcol[:, :]), reads=[lbcol], writes=[lbcol])
        O.tt("dve", lbcol(), lbcol(), lbe(A_, slice(1, 2)), ALU.mult)
        O.ts("dve", omlb(), lbcol(), -1.0, ALU.mult, 1.0, ALU.add)
        tmp_open()
        stg1 = sb("stg1", [128, KC, 512])
        for hh in range(2):
            O.ld(stg1(), w_in_b[:, hh * 512:(hh + 1) * 512].rearrange("(kc p) n -> p kc n", p=128))
            O.copy("pool" if hh else "dve", wb(A_, A_, slice(hh * 512, hh * 512 + 512)), stg1())
        tmp_close()
        X1 = sb("X1", [128, KC, NT])
        Q1 = sb("Q1", [128, KC, NT], BF16)
        H1 = sb("H1", [128, KC, NT], BF16)
        L1 = sb("L1", [128, NT])
        R1 = sb("R1", [128, NT])
        K1s = [sb(f"K1s{i}", [128, NT]) for i in range(2)]
        V1s = [sb(f"V1s{i}", [128, 4, 2, 64]) for i in range(2)]
        qT1 = sb("qT1", [128, NT], BF16)
        lft = sb("lft", [128, 8])
        lfo = [sb(f"lfo{i}", [128, 8]) for i in range(2)]
        pb_ = [[sb(f"pb_{h}{p}", [128, NT], BF16) for p in range(2)] for h in range(2)]
        oc_t = [sb(f"oc_t{h}", [128, NT]) for h in range(2)]
        rlt = [sb(f"rlt{h}", [128, NT]) for h in range(2)]
        mixfx = [[sb(f"mixfx{h}{p}", [64, NT], BF16) for p in range(2)] for h in range(2)]
        sigf, fg, kk, lgf, bcs, eb, enb, qs, exb, gs, onn = (sb(nm, [128, NT]) for nm in
                                                              ("sigf", "fg", "kk", "lgf", "bcs", "eb", "enb", "qs", "exb", "gs", "onn"))
        qt = sb("qt", [128, NT], BF16)
        kt = sb("kt", [128, NT], BF16)
        khT = sb("khT", [128, NT], BF16)
        osq = sb("osq", [128, NT], BF16)
        ebend = sb("ebend", [128, 8])
        vtm = sb("vtm", [128, 4, 128], BF16)
        vtms = [sb(f"vtms{i}", [DS, 128], BF16) for i in range(NSB)]
        scm = sb("scm", [128, 64], BF16)
        khtm = sb("khtm", [128, 128], BF16)
        hgo = [sb(f"hgo{i}", [128, NT], BF16) for i in range(2)]
        knew1 = [sb(f"knew1{i}", [128, 128], BF16) for i in range(NSB)]
        vnew1 = [sb(f"vnew1{i}", [128, 2, 65], BF16) for i in range(NSB)]
        lfn = [sb(f"lfn{i}", [DS, 2]) for i in range(NSB)]
        vfs = sb("vfs", [DS, 128])
        kcs1 = [sb(f"kcs1{i}", [128, 512]) for i in range(2)]
        kcb1 = [sb(f"kcb1{i}", [128, 512], BF16) for i in range(2)]
        vcs1 = [sb(f"vcs1{i}", [128, 4, 2, 64]) for i in range(2)]
        vcb1 = [sb(f"vcb1{i}", [128, 4, 2, 65], BF16) for i in range(2)]
        for i in range(2):
            O.memset("pool", vcb1[i](), 1.0)
        flct = sb("flct", [128, NCB, 2])
        identb = tri_bf(A_, 3, A_)
        gncol = hgpt(A_, slice(2, 3))

        def logf_block(lf_op, rows, dst, cr):
            cb_ = bank[7]
            O.mm(cb_(slice(0, rows), slice(0, 2)), trif(slice(0, rows), slice(0, rows)), lf_op)
            O.tt("dve", dst, cb_(slice(0, rows), slice(0, 2)), cr(slice(0, rows), A_), ALU.add)
            O.mm(cb_(A_, slice(8, 10)), onesf(slice(0, rows), slice(0, 128)), lf_op)
            O.tt("dve", cr(), cb_(A_, slice(8, 10)), cr(), ALU.add)

        def hg_gen(chunks, vfn, scb, ktb, kc0, snb, ob):
            cur_seq = None
            for ci, (c0, L, po, sbi, seq) in enumerate(chunks):
                if seq != cur_seq:
                    O.copy("pool", Sbf(), Sst(A_, seq, A_))
                    cur_seq = seq
                vop = vfn(c0, L, po, sbi)
                cs = slice(c0, c0 + L)
                pr_ = slice(po, po + L)
                O.mm(scb(pr_, slice(0, L)), kt(A_, cs), qt(A_, cs), skip=True)
                yield
                O.tt("dve", scm(pr_, slice(0, L)), scb(pr_, slice(0, L)), masks_bf(pr_, 1, slice(384 + po, 384 + po + L)), ALU.mult)
                yield
                O.mm(ob(A_, cs), vop, scm(pr_, slice(0, L)), start=True, stop=False, skip=True)
                O.mm(ob(A_, cs), Sbf(), qt(A_, cs), start=False, stop=True, skip=True)
                yield
                O.mm(ktb(pr_, slice(kc0, kc0 + 128)), khT(A_, cs), identb, skip=True)
                yield
                O.act(khtm(pr_, A_), ktb(pr_, slice(kc0, kc0 + 128)), AF.Copy)
                yield
                O.mm(snb(A_, slice(0, 128)), khtm(pr_, A_), vop, skip=True)
                yield
                O.stt(Sst(A_, seq, A_), Sst(A_, seq, A_), ebend(A_, slice(ci, ci + 1)), snb(A_, slice(0, 128)), ALU.mult, ALU.add)
                yield
                O.copy("pool", Sbf(), Sst(A_, seq, A_))
                yield

        def fox_attend(qfn, n, blocks, outs, oc0, side=None):
            ns = slice(0, n)
            zb1, Ob1, bcb = [[bank[1], bank[7]], [bank[2], bank[0]]], [bank[5], bank[6]], bank[3]
            if side is not None:
                zb1 = [[bank[1], bank[1]], [bank[2], bank[2]]]
            nb = len(blocks)
            pulls = 1 if nb >= 64 else -(-64 // nb)
            info = {}

            def stA(i):
                info[i] = blocks[i]()
                kf = info[i][0]
                for hd in range(2):
                    O.mm(zb1[hd][i % 2](A_, ns), kf(hd), qfn(hd))

            def stB(i):
                mk, bi = info[i][2], info[i][3]
                for hd in range(2):
                    Pt = pb_[hd][i % 2]
                    O.act(Pt(A_, ns), zb1[hd][i % 2](A_, ns), AF.Exp, bias=btab(A_, slice(bi, bi + 1), hd))
                    if mk is not None:
                        O.tt("pool", Pt(A_, ns), Pt(A_, ns), mk, ALU.mult)

            def stE(i):
                vf = info[i][1]
                for hd in range(2):
                    O.mm(Ob1[hd](slice(0, 65), ns), vf(hd), pb_[hd][i % 2](A_, ns), start=(i == 0), stop=(i == nb - 1))

            la = 2 if side is None else 1
            for i0 in range(min(la, nb)):
                stA(i0)
            for i in range(nb):
                stB(i)
                if i + la < nb:
                    stA(i + la)
                stE(i)
                if side is not None:
                    for _ in range(pulls):
                        next(side, None)
            if side is not None:
                for _ in side:
                    pass
            for hd in range(2):
                O.act(oc_t[hd](slice(0, 65), ns), Ob1[hd](slice(0, 65), ns), AF.Copy)
                S.op("dve", lambda e, hd=hd: e.reciprocal(out=rlt[hd][64:65, :n], in_=oc_t[hd][64:65, :n]),
                     reads=[oc_t[hd]], writes=[rlt[hd]])
                O.mm(bcb(slice(0, 64), ns), onesf(slice(64, 65), slice(0, 64)), rlt[hd](slice(64, 65), ns))
                O.tt("dve", outs[hd](A_, slice(oc0, oc0 + n)), oc_t[hd](slice(0, 64), ns), bcb(slice(0, 64), ns), ALU.mult)

        def layer1_tile(src_ap, n, it, okT_ap, ov_ap, olf_ap, prompt_j):
            b = it % 2
            ns = slice(0, n)
            MBi = MB1[it]
            S.dma("sp", X1[:, :, :n], src_ap.rearrange("(kc p) n -> p kc n", p=128), X1, writes=[X1])
            O.act(Q1(A_, A_, ns), X1(A_, A_, ns), AF.Square)
            for kc in range(KC):
                O.mm(bank[0](A_, ns), ones_bf(), Q1(A_, kc, ns), start=(kc == 0), stop=(kc == KC - 1))
            O.act(L1(A_, ns), bank[0](A_, ns), AF.Ln, scale=1.0 / D, bias=EPS)
            O.act(R1(A_, ns), L1(A_, ns), AF.Exp, scale=-0.5)
            for kc in range(KC):
                O.stt(H1(A_, kc, ns), X1(A_, kc, ns), gm1(A_, slice(kc, kc + 1)), R1(A_, ns), ALU.mult, ALU.mult)
            H = H1
            proj_fm(H, n, wb, 0, bank[1])
            proj_fm(H, n, wb, 128, bank[2])
            proj_fm(H, n, wb, 384, bank[3])
            O.act(sigf(A_, ns), bank[2](A_, ns), AF.Sigmoid)
            O.act(qs(A_, ns), bank[1](A_, ns), AF.Silu)
            O.act(gs(A_, ns), bank[3](A_, ns), AF.Silu)
            O.ts("dve", fg(A_, ns), sigf(A_, ns), omlb(A_, slice(0, 1)), ALU.mult, lbcol(A_, slice(0, 1)), ALU.add)
            O.ts("dve", kk(A_, ns), fg(A_, ns), -1.0, ALU.mult, 1.0, ALU.add)
            O.act(lgf(A_, ns), fg(A_, ns), AF.Ln)
            if prompt_j is not None:
                chunks = [(64 * c, 64, 64 * (c % 2), None, 0) for c in range(n // 64)]
            else:
                chunks = [(DS * i, DS, 0, i, 1 + i) for i in range(NSB)]
            for (c0, L, po, sbi, seq) in chunks:
                O.scan(bcs(A_, slice(c0, c0 + L)), onesf(A_, slice(0, L)), lgf(A_, slice(c0, c0 + L)), 0.0)
            O.act(eb(A_, ns), bcs(A_, ns), AF.Exp)
            O.act(enb(A_, ns), bcs(A_, ns), AF.Exp, scale=-1.0)
            O.tt("dve", qt(A_, ns), qs(A_, ns), eb(A_, ns), ALU.mult)
            O.tt("dve", kt(A_, ns), kk(A_, ns), enb(A_, ns), ALU.mult)
            for ci, (c0, L, po, sbi, seq) in enumerate(chunks):
                e_ = slice(c0 + L - 1, c0 + L)
                O.act(exb(A_, slice(c0, c0 + L)), bcs(A_, slice(c0, c0 + L)), AF.Exp, scale=-1.0, bias=bcs(A_, e_))
                O.copy("pool", ebend(A_, slice(ci, ci + 1)), eb(A_, e_))
            O.tt("dve", khT(A_, ns), kk(A_, ns), exb(A_, ns), ALU.mult)
            if prompt_j is not None:
                proj_tm(H, n, wb, 256, bank[5])
                S.op("dve", lambda e: e.tensor_copy(out=vtm[:, :, :], in_=bank[5][:, :].rearrange("p (s c) -> p s c", c=128)),
                     reads=[bank[5]], writes=[vtm])
            else:
                for i in range(NSB):
                    for kc in range(KC):
                        O.mm(bank[5](slice(0, DS), slice(128 * i, 128 * i + 128)), H(A_, kc, slice(DS * i, DS * i + DS)),
                             wb(A_, kc, slice(256, 384)), start=(kc == 0), stop=(kc == KC - 1))
                    O.copy("dve", vtms[i](), bank[5](slice(0, DS), slice(128 * i, 128 * i + 128)))
            vfn = lambda c0, L, po, sbi: (vtm(slice(po, po + L), c0 // 128, A_) if sbi is None else vtms[sbi]())

            def hg_post():
                ob = bank[4]
                O.act(osq(A_, ns), ob(A_, ns), AF.Square)
                O.mm(bank[5](A_, ns), ones_bf(), osq(A_, ns))
                O.act(L1(A_, ns), bank[5](A_, ns), AF.Ln, scale=1.0 / 128, bias=EPS)
                O.act(R1(A_, ns), L1(A_, ns), AF.Exp, scale=-0.5)
                O.tt("dve", onn(A_, ns), ob(A_, ns), R1(A_, ns), ALU.mult)
                O.stt(hgo[b](A_, ns), onn(A_, ns), gncol, gs(A_, ns), ALU.mult, ALU.mult)
                S.dma("pool", MBi[0:128, 0:n], hgo[b][:, :n], hgo[b], reads=[hgo[b]])

            if prompt_j is not None:
                side = hg_gen(chunks, vfn, bank[7], bank[7], 128, bank[0], bank[4])
            else:
                side = None
                for _ in hg_gen(chunks, vfn, bank[1], bank[2], 0, bank[3], bank[4]):
                    pass
                hg_post()
            K = K1s[b]
            proj_fm(H, n, wb, 640, bank[1])
            O.act(K(A_, ns), bank[1](A_, ns), AF.Copy)
            O.st(okT_ap, K(A_, ns))
            proj_fm(H, n, wb, 512, bank[2])
            O.act(qT1(A_, ns), bank[2](A_, ns), AF.Copy, scale=0.125)
            ms = [mixfx[0][b], mixfx[1][b]]
            V = V1s[b]
            if prompt_j is not None:
                j = prompt_j
                proj_tm(H, n, wb, 768, bank[3])
                S.op("dve", lambda e: e.tensor_copy(out=V[:, :, :, :], in_=bank[3][:, :].rearrange("p (s h c) -> p s h c", h=2, c=64)),
                     reads=[bank[3]], writes=[V])
                S.dma("pool", ov_ap.rearrange("(s p) c -> p s c", p=128), V[:, :, :, :].rearrange("p s h c -> p s (h c)"), V, reads=[V])
                O.copy("pool", k1_all(A_, slice(j * NT, j * NT + n)), K(A_, ns))
                O.copy("pool", v1_all(A_, slice(4 * j, 4 * j + 4), A_, slice(0, 64)), V())
                for s_ in range(4):
                    for kc in range(KC):
                        O.mm(bank[7](A_, slice(16 + 2 * s_, 18 + 2 * s_)), H(A_, kc, slice(128 * s_, 128 * s_ + 128)),
                             wb(A_, kc, slice(896, 898)), start=(kc == 0), stop=(kc == KC - 1), skip=True)
                O.tt("dve", lft(), bank[7](A_, slice(16, 24)), bfr(), ALU.add)
                O.act(lft(), lft(), AF.Exp, scale=-1.0)
                O.act(lft(), lft(), AF.Ln, bias=1.0)
                LF = lfo[b]
                O.ts("dve", LF(), lft(), -1.0, ALU.mult)
                S.dma("pool", olf_ap.rearrange("(s p) h -> p s h", p=128), LF[:, :].rearrange("p (s h) -> p s h", h=2), LF, reads=[LF])
                for s_ in range(4):
                    if s_ == 2:
                        O.copy("dve", shq(), carry())
                    logf_block(LF(A_, slice(2 * s_, 2 * s_ + 2)), 128, cKM(A_, 4 * j + s_, A_), carry)
                nkb = 4 * j + 4
                for hd in range(2):
                    O.ts("dve", btab(A_, slice(0, nkb), hd), cKM(A_, slice(0, nkb), hd), -1.0, ALU.mult, shq(A_, slice(hd, hd + 1)), ALU.add)
                blocks = []
                for kb_ in range(nkb - 1, -1, -1):
                    m_ = kb_ - 4 * j
                    mk = masks_bf(A_, 1, slice(384 - 128 * m_, 896 - 128 * m_)) if m_ >= 0 else None
                    blocks.append(lambda kb_=kb_, mk=mk: (
                        lambda hd: k1_all(slice(64 * hd, 64 * hd + 64), slice(kb_ * 128, kb_ * 128 + 128)),
                        lambda hd: v1_all(A_, kb_, hd, A_), mk, kb_))
                fox_attend(lambda hd: qT1(slice(64 * hd, 64 * hd + 64), ns), n, blocks, ms, 0, side=side)
                hg_post()
            else:
                for i in range(NSB):
                    c0 = DS * i
                    cs = slice(c0, c0 + DS)
                    O.memset("pool", knew1[i](), 0.0)
                    O.memset("pool", vnew1[i](), 0.0)
                    O.copy("pool", knew1[i](A_, slice(0, DS)), K(A_, cs))
                    for kc in range(KC):
                        O.mm(bank[3](slice(0, DS), slice(0, 128)), H(A_, kc, cs), wb(A_, kc, slice(768, 896)),
                             start=(kc == 0), stop=(kc == KC - 1))
                    O.copy("dve", vfs(), bank[3](slice(0, DS), slice(0, 128)))
                    S.dma("pool", ov_ap[c0:c0 + DS, :], vfs[:, :], vfs, reads=[vfs])
                    S.op("pool", lambda e, i=i: e.tensor_copy(out=vnew1[i][0:DS, :, 0:64], in_=vfs[:, :].rearrange("p (h c) -> p h c", h=2)),
                         reads=[vfs], writes=[vnew1[i]])
                    O.memset("pool", vnew1[i](slice(0, DS), A_, slice(64, 65)), 1.0)
                    for kc in range(KC):
                        O.mm(bank[7](slice(0, DS), slice(16, 18)), H(A_, kc, cs), wb(A_, kc, slice(896, 898)),
                             start=(kc == 0), stop=(kc == KC - 1), skip=True)
                    O.tt("dve", lft(slice(0, DS), slice(0, 2)), bank[7](slice(0, DS), slice(16, 18)), bfr(slice(0, DS), slice(0, 2)), ALU.add)
                    O.act(lft(slice(0, DS), slice(0, 2)), lft(slice(0, DS), slice(0, 2)), AF.Exp, scale=-1.0)
                    O.act(lft(slice(0, DS), slice(0, 2)), lft(slice(0, DS), slice(0, 2)), AF.Ln, bias=1.0)
                    O.ts("dve", lfn[i](), lft(slice(0, DS), slice(0, 2)), -1.0, ALU.mult)
                    S.dma("pool", olf_ap[c0:c0 + DS, :], lfn[i][:, :], lfn[i], reads=[lfn[i]])
                    O.ld(flct(), flc[:, i, :, :])
                    O.memset("dve", carry(), 0.0)
                    O.memset("dve", cKMs(), 0.0)
                    for kb_ in range(NCB):
                        logf_block(flct(A_, kb_, A_), 128, cKMs(A_, kb_, A_), carry)
                    O.copy("dve", shq(), carry())
                    logf_block(lfn[i](), DS, cKMs(slice(0, DS), NCB, A_), carry)
                    for hd in range(2):
                        O.ts("dve", btab(A_, slice(0, NCB + 1), hd), cKMs(A_, slice(0, NCB + 1), hd), -1.0, ALU.mult,
                             shq(A_, slice(hd, hd + 1)), ALU.add)
                    blocks = [lambda i=i: (lambda hd: knew1[i](slice(64 * hd, 64 * hd + 64), A_),
                                           lambda hd: vnew1[i](A_, hd, A_),
                                           masks_bf(A_, 1, slice(384, 384 + DS)), NCB)]
                    for kb_ in range(NCB - 1, -1, -1):
                        def blk(kb_=kb_, i=i):
                            ci, sub = kb_ // 4, kb_ % 4
                            pr_ = ci % 2
                            if sub == 3:
                                O.ld(kcs1[pr_](), fkc[:, i, ci * 512:(ci + 1) * 512])
                                O.copy("pool", kcb1[pr_](), kcs1[pr_]())
                                S.dma("sp", vcs1[pr_][:, :, :, :].rearrange("p s h c -> p s (h c)"),
                                      fvc[ci * 512:(ci + 1) * 512, i, :].rearrange("(s p) c -> p s c", p=128), vcs1[pr_], writes=[vcs1[pr_]])
                                O.copy("pool", vcb1[pr_](A_, A_, A_, slice(0, 64)), vcs1[pr_]())
                            return (lambda hd: kcb1[pr_](slice(64 * hd, 64 * hd + 64), slice(sub * 128, sub * 128 + 128)),
                                    lambda hd: vcb1[pr_](A_, sub, hd, A_), None, kb_)
                        blocks.append(blk)
                    fox_attend(lambda hd, cs=cs: qT1(slice(64 * hd, 64 * hd + 64), cs), DS, blocks, ms, c0)
            for hd in range(2):
                S.dma("pool", MBi[128 + 64 * hd:192 + 64 * hd, 0:n], ms[hd][:, :n], ms[hd], reads=[ms[hd]])
            exchange(MBi, MG1[it], [hgo[b], ms[0], ms[1]])

        it = 0
        for j in range(T // NT):
            cl = slice(j * NT, (j + 1) * NT)
            layer1_tile(XS[:, cl], NT, it, o_fkT[:, cl], o_fv[cl, :], o_flf[cl, :], j)
            it += 1
        layer1_tile(XS[:, T:T + NS], NS, it, o_fkT_s[:, :], o_fv_s[:, :], o_flf_s[:, :], None)
        S.dma("sp", o_hg[:, :], Sst[:, :, :].rearrange("p s v -> p (s v)"), Sst, reads=[Sst])
        cut('l1mix')
        tmp_close()
        run_phaseC(1, MG1)

    except _Cut:
        pass
    S.finish()
    with nc.Block() as block:
        S.emit(block)
    while len(scopes) > 1:
        scopes.pop().close()
    es.close()
    return nc


_NC_CACHE = {}
OUT_NAMES = ["o_kT", "o_v", "o_kT_s", "o_v_s", "o_s5", "o_conv0", "o_conv1", "o_y", "o_fkT", "o_fv", "o_flf", "o_fkT_s", "o_fv_s", "o_flf_s", "o_hg"]


def _bc(row):
    return np.ascontiguousarray(np.broadcast_to(np.asarray(row, np.float32)[None, :], (128, row.shape[0])))


def kernel(**inp):
    f = np.float32
    x_prompt = np.asarray(inp["x_prompt"], f)
    x_sample = np.asarray(inp["x_sample"], f)
    B, T, _ = x_prompt.shape
    P = inp["cache_sb_k"].shape[2]
    key = (T, P)
    if key not in _NC_CACHE:
        _NC_CACHE[key] = build(T, P)
    nc = _NC_CACHE[key]
    g = lambda k: np.asarray(inp[k], f)

    in_maps = []
    for c in range(8):
        b, q = c // 4, c % 4
        m = {}
        m["xT_p"] = np.ascontiguousarray(x_prompt[b].T)
        m["xT_s"] = np.ascontiguousarray(x_sample[4 * b:4 * b + 4].reshape(NSB * DS, D).T)
        m["g_mix0"] = np.ascontiguousarray(g("norm_mix_g")[0].reshape(KC, 128).T)
        wa = g("w_in_a")[0]
        cols = np.concatenate([np.arange(128 * q, 128 * q + 128) + off for off in (0, 512, 1024, 1536)])
        m["w_in_a"] = np.ascontiguousarray(wa[:, cols])
        G = slice(8 * q, 8 * q + 8)
        lre, lim = g("s5_lambda_re")[0, G], g("s5_lambda_im")[0, G]
        ldt = np.repeat(g("s5_log_dt")[0, G][:, None], 64, axis=1)
        flat = lambda a: a.reshape(4, 128)
        m["s5col"] = np.ascontiguousarray(np.stack([flat(lre).T, flat(lim).T, flat(ldt).T], axis=1))
        m["s5row"] = np.ascontiguousarray(np.stack([_bc(lre.reshape(-1)), _bc(lim.reshape(-1)), _bc(ldt.reshape(-1))], axis=1))
        Bb = np.zeros((128, 2, 4, 128), f)
        Cb = np.zeros((128, 2, 4, 128), f)
        bre, bim = g("s5_b_re")[0, G], g("s5_b_im")[0, G]
        cre, cim = g("s5_c_re")[0, G], g("s5_c_im")[0, G]
        for gl in range(8):
            j, hf = gl // 2, gl % 2
            Bb[16 * gl:16 * gl + 16, 0, j, 64 * hf:64 * hf + 64] = bre[gl].T
            Bb[16 * gl:16 * gl + 16, 1, j, 64 * hf:64 * hf + 64] = bim[gl].T
            Cb[64 * hf:64 * hf + 64, 0, j, 16 * gl:16 * gl + 16] = cre[gl].T
            Cb[64 * hf:64 * hf + 64, 1, j, 16 * gl:16 * gl + 16] = cim[gl].T
        m["s5B"] = Bb.reshape(128, 2, 512)
        m["s5C"] = Cb.reshape(128, 2, 512)
        m["s5d"] = np.ascontiguousarray(g("s5_d")[0, 128 * q:128 * q + 128].reshape(128, 1))
        hre0 = g("state_s5_re")[0, 4 * b:4 * b + 4, G]
        him0 = g("state_s5_im")[0, 4 * b:4 * b + 4, G]
        h0 = np.stack([hre0.reshape(NSB, 4, 128), him0.reshape(NSB, 4, 128)], axis=0)
        m["s5h0"] = np.ascontiguousarray(h0.transpose(3, 0, 1, 2))
        jj = np.arange(128)[:, None]
        cc = np.arange(896)[None, :] - 384
        m["cmask"] = np.ascontiguousarray(np.stack([(jj < cc), (jj <= cc)], axis=1).astype(f))
        kk_ = np.arange(128)[None, :]
        m["ctri"] = np.ascontiguousarray(np.stack([(jj >= kk_), (jj < kk_), (jj <= kk_), (jj == kk_)], axis=1).astype(f))
        ck = g("cache_sb_k")[0, 4 * b:4 * b + 4, :, 2 * q:2 * q + 2, :]
        cv = g("cache_sb_v")[0, 4 * b:4 * b + 4, :, 2 * q:2 * q + 2, :]
        m["sbkc"] = np.ascontiguousarray(ck.transpose(2, 3, 0, 1).reshape(128, NSB, P))
        m["sbvc"] = np.ascontiguousarray(cv.transpose(1, 0, 2, 3).reshape(P, NSB, 128))
        perm = np.concatenate([np.concatenate([np.arange(128 * r, 128 * r + 128), 512 + np.arange(128 * r, 128 * r + 128)]) for r in range(4)])
        m["w_out0"] = np.ascontiguousarray(g("w_out_a")[0][perm, :])
        m["w_out1"] = np.ascontiguousarray(g("w_out_b")[0][perm, :])
        m["g_fin"] = np.ascontiguousarray(g("final_norm_g").reshape(KC, 128).T)
        m["w_glu"] = np.ascontiguousarray(g("s5_w_glu")[0])
        m["b_glu"] = np.ascontiguousarray(g("s5_b_glu")[0].reshape(4, 128).T)
        m["g_mix1"] = np.ascontiguousarray(g("norm_mix_g")[1].reshape(KC, 128).T)
        wbf = g("w_in_b")[0]
        cols1 = np.concatenate([np.arange(128 * q, 128 * q + 128) + off for off in (0, 512, 1024, 1536, 2048, 2560, 3072)]
                               + [np.array([3584 + 2 * q, 3584 + 2 * q + 1])])
        wpad = np.zeros((D, 1024), f)
        wpad[:, :898] = wbf[:, cols1]
        m["w_in_b"] = wpad
        lbl = g("hg_lb_logits")[:, 128 * q:128 * q + 128]
        m["hgp"] = np.ascontiguousarray(np.stack([lbl[0], lbl[1], g("hg_norm_g")[0, 128 * q:128 * q + 128]], axis=1))
        bf2 = g("fox_b_f")[0, 2 * q:2 * q + 2]
        m["bfrow"] = np.ascontiguousarray(np.broadcast_to(np.tile(bf2, 4)[None, :], (128, 8)))
        m["hgS0"] = np.ascontiguousarray(g("state_hgrn")[0, 4 * b:4 * b + 4, q].transpose(1, 0, 2))
        fk_ = g("cache_fox_k")[0, 4 * b:4 * b + 4, :, 2 * q:2 * q + 2, :]
        fv_ = g("cache_fox_v")[0, 4 * b:4 * b + 4, :, 2 * q:2 * q + 2, :]
        m["fkc"] = np.ascontiguousarray(fk_.transpose(2, 3, 0, 1).reshape(128, NSB, P))
        m["fvc"] = np.ascontiguousarray(fv_.transpose(1, 0, 2, 3).reshape(P, NSB, 128))
        fl_ = g("cache_fox_logf")[0, 4 * b:4 * b + 4, :, 2 * q:2 * q + 2]
        m["flc"] = np.ascontiguousarray(fl_.reshape(NSB, P // 128, 128, 2).transpose(2, 0, 1, 3))
        NFL_ = 6
        chs = [6 * q + s_ for s_ in range(NFL_)]
        for l in range(2):
            m[f"g_ffn{l}"] = np.ascontiguousarray(g("norm_ffn_g")[l].reshape(KC, 128).T)
            wu = g("ffn_w_up")[l]
            wd = g("ffn_w_down")[l]
            cw = np.concatenate([g("ffn_conv_w")[l], g("ffn_conv_b")[l][None, :]], axis=0)
            cs_ = g("state_ffn_conv")[l, 4 * b:4 * b + 4]
            wul = np.zeros((D, NFL_, 256), f)
            wdl = np.zeros((NFL_ * 128, D), f)
            cwl = np.zeros((128, NFL_, 4), f)
            csl = np.zeros((128, NSB, NFL_, 2), f)
            for s_, ch in enumerate(chs):
                if ch >= 22:
                    continue
                cl = slice(128 * ch, 128 * ch + 128)
                wul[:, s_, :128] = wu[:, cl]
                wul[:, s_, 128:] = wu[:, 2816 + 128 * ch:2816 + 128 * ch + 128]
                wdl[128 * s_:128 * s_ + 128] = wd[cl]
                cwl[:, s_, :] = cw[:, cl].T
                csl[:, :, s_, :] = cs_[:, :, cl].transpose(2, 0, 1)
            m[f"w_up{l}"] = wul
            m[f"w_dn{l}"] = wdl
            m[f"cw{l}"] = cwl
            m[f"cst{l}"] = csl
        in_maps.append(m)

    res = run_bass_kernel_spmd(nc, in_maps, core_ids=list(range(8)))
    R = res.results

    NE, NO = 1, 1
    p_sb_k = np.zeros((NE, B, T, 8, 64), f)
    p_sb_v = np.zeros((NE, B, T, 8, 64), f)
    s_sb_k = np.zeros((NE, 8, DS, 8, 64), f)
    s_sb_v = np.zeros((NE, 8, DS, 8, 64), f)
    p_s5 = np.zeros((2, NE, B, 32, 64), f)
    s_s5 = np.zeros((2, NE, 8, 32, 64), f)
    for c in range(8):
        b, q = c // 4, c % 4
        r = R[c]
        p_sb_k[0, b, :, 2 * q:2 * q + 2, :] = r["o_kT"].reshape(128, T).T.reshape(T, 2, 64)
        p_sb_v[0, b, :, 2 * q:2 * q + 2, :] = r["o_v"].reshape(T, 2, 64)
        s_sb_k[0, 4 * b:4 * b + 4, :, 2 * q:2 * q + 2, :] = r["o_kT_s"].reshape(128, NSB * DS).T.reshape(NSB, DS, 2, 64)
        s_sb_v[0, 4 * b:4 * b + 4, :, 2 * q:2 * q + 2, :] = r["o_v_s"].reshape(NSB, DS, 2, 64)
        st = r["o_s5"].reshape(128, 1 + NSB, 2, 4)
        st = st.transpose(1, 2, 3, 0).reshape(1 + NSB, 2, 8, 64)
        for ri in range(2):
            p_s5[ri, 0, b, 8 * q:8 * q + 8] = st[0, ri]
            s_s5[ri, 0, 4 * b:4 * b + 4, 8 * q:8 * q + 8] = st[1:, ri]

    p_fk = np.zeros((NO, B, T, 8, 64), f)
    p_fv = np.zeros((NO, B, T, 8, 64), f)
    p_fl = np.zeros((NO, B, T, 8), f)
    s_fk = np.zeros((NO, 8, DS, 8, 64), f)
    s_fv = np.zeros((NO, 8, DS, 8, 64), f)
    s_fl = np.zeros((NO, 8, DS, 8), f)
    p_hg = np.zeros((NO, B, 4, 128, 128), f)
    s_hg = np.zeros((NO, 8, 4, 128, 128), f)
    p_cv = np.zeros((2, B, 2, 2816), f)
    s_cv = np.zeros((2, 8, 2, 2816), f)
    y_p = np.zeros((B, T, D), f)
    y_s = np.zeros((8, DS, D), f)
    NS_ = NSB * DS
    for c in range(8):
        b, q = c // 4, c % 4
        r = R[c]
        p_fk[0, b, :, 2 * q:2 * q + 2, :] = r["o_fkT"].reshape(128, T).T.reshape(T, 2, 64)
        p_fv[0, b, :, 2 * q:2 * q + 2, :] = r["o_fv"].reshape(T, 2, 64)
        p_fl[0, b, :, 2 * q:2 * q + 2] = r["o_flf"].reshape(T, 2)
        s_fk[0, 4 * b:4 * b + 4, :, 2 * q:2 * q + 2, :] = r["o_fkT_s"].reshape(128, NS_).T.reshape(NSB, DS, 2, 64)
        s_fv[0, 4 * b:4 * b + 4, :, 2 * q:2 * q + 2, :] = r["o_fv_s"].reshape(NSB, DS, 2, 64)
        s_fl[0, 4 * b:4 * b + 4, :, 2 * q:2 * q + 2] = r["o_flf_s"].reshape(NSB, DS, 2)
        hg = r["o_hg"].reshape(128, 1 + NSB, 128).transpose(1, 0, 2)
        p_hg[0, b, q] = hg[0]
        s_hg[0, 4 * b:4 * b + 4, q] = hg[1:]
        for l in range(2):
            ac = r[f"o_conv{l}"].reshape(128, 1 + NSB, 6, 2).transpose(1, 3, 2, 0)
            for s_ in range(6):
                ch = 6 * q + s_
                if ch < 22:
                    p_cv[l, b, :, 128 * ch:128 * ch + 128] = ac[0, :, s_, :]
                    s_cv[l, 4 * b:4 * b + 4, :, 128 * ch:128 * ch + 128] = ac[1:, :, s_, :]
        if q == 0:
            yT = r["o_y"].reshape(D, T + NS_)
            y_p[b] = yT[:, :T].T
            y_s[4 * b:4 * b + 4] = yT[:, T:].T.reshape(NSB, DS, D)

    return (y_p, y_s,
            p_sb_k, p_sb_v, p_s5[0], p_s5[1], p_fk, p_fv, p_fl, p_hg, p_cv,
            s_sb_k, s_sb_v, s_s5[0], s_s5[1], s_fk, s_fv, s_fl, s_hg, s_cv)
```

```python
import contextlib
import os
SEM_ROT = int(os.environ.get("SEM_ROT", "30000"))
KSTAGE = os.environ.get('KSTAGE', 'all')
DBG_MIX = os.environ.get('DBG_MIX', '0') == '1'
import numpy as np
import concourse.bass as bass
import concourse.mybir as mybir
from concourse.bass_utils import run_bass_kernel_spmd

F32 = mybir.dt.float32
BF16 = mybir.dt.bfloat16
ALU = mybir.AluOpType
AF = mybir.ActivationFunctionType

D = 1024
KC = 8
NT = 512
EPS = 1e-6
T_FULL = 16384
P_FULL = 4096
DS = 16
NSB = 4


class Trk:
    def __init__(self, ap=None, name=""):
        self.ap = ap
        self.name = name
        self.w = None
        self.rs = {}
        self.dsem = None
        self.dcnt = 0

    def __getitem__(self, idx):
        return self.ap[idx]


class Sched:
    ENG = ("pe", "act", "dve", "pool", "sp")

    def __init__(self, nc):
        self.nc = nc
        self.eobj = {"pe": nc.tensor, "act": nc.scalar, "dve": nc.vector, "pool": nc.gpsimd, "sp": nc.sync}
        self.q = {e: [] for e in self.ENG}
        self.sem = {}
        self.cnt = {}
        self.waited = {e: {} for e in self.ENG}
        self.nsem = 0
        for e in self.ENG:
            self._new_sem(e)
        self.dma_tiles = []

    def _alloc(self, name):
        self.nsem += 1
        return self.nc.alloc_semaphore(name=f"{name}_{self.nsem}")

    def _new_sem(self, e):
        self.sem[e] = self._alloc("e" + e)
        self.cnt[e] = 0

    def _wait(self, eng, deps):
        for d in deps:
            if d is None:
                continue
            kind, a, b, src = d
            if kind == "c":
                sem, val = a, b
                if src == eng == "pe":
                    continue
            else:
                sem, val = a.dsem, a.dcnt
            key = id(sem)
            if self.waited[eng].get(key, 0) < val:
                self.q[eng].append(("w", sem, val))
                self.waited[eng][key] = val

    def _deps(self, reads, writes):
        deps = []
        for t in reads:
            deps.append(t.w)
        for t in writes:
            deps.append(t.w)
            deps.extend(t.rs.values())
        return deps

    def _mark(self, me, key, reads, writes):
        for t in reads:
            t.rs[key] = me
        for t in writes:
            t.w = me
            t.rs = {}

    def op(self, eng, fn, reads=(), writes=()):
        self._wait(eng, self._deps(reads, writes))
        if self.cnt[eng] >= SEM_ROT:
            self._new_sem(eng)
        self.cnt[eng] += 1
        sem = self.sem[eng]
        me = ("c", sem, self.cnt[eng], eng)
        self.q[eng].append(("i", fn, sem))
        self._mark(me, id(sem), reads, writes)

    def dma(self, eng, out, in_, dt, reads=(), writes=(), **kw):
        self._wait(eng, self._deps(reads, writes))
        if dt.dsem is None:
            dt.dsem = self._alloc("d")
            self.dma_tiles.append(dt)
        dt.dcnt += 16
        me = ("d", dt, None, eng)
        self.q[eng].append(("d", lambda e, o=out, i=in_, k=kw: e.dma_start(out=o, in_=i, **k), dt.dsem))
        self._mark(me, ("d", id(dt)), reads, writes)

    def raw(self, eng, fn, sem, inc, reads=(), writes=(), dt=None):
        self._wait(eng, self._deps(reads, writes))
        dt.dcnt += inc
        me = ("d", dt, None, eng)
        self.q[eng].append(("r", fn, sem, inc))
        self._mark(me, ("d", id(dt)), reads, writes)

    def barrier(self):
        for e in self.ENG:
            for o in self.ENG:
                if o != e and self.cnt[o] > 0:
                    key = id(self.sem[o])
                    if self.waited[e].get(key, 0) < self.cnt[o]:
                        self.q[e].append(("w", self.sem[o], self.cnt[o]))
                        self.waited[e][key] = self.cnt[o]
            for dt in self.dma_tiles:
                key = id(dt.dsem)
                if self.waited[e].get(key, 0) < dt.dcnt:
                    self.q[e].append(("w", dt.dsem, dt.dcnt))
                    self.waited[e][key] = dt.dcnt

    def wait_tiles(self, eng, tiles):
        for dt in tiles:
            if dt.dsem is None:
                continue
            key = id(dt.dsem)
            if self.waited[eng].get(key, 0) < dt.dcnt:
                self.q[eng].append(("w", dt.dsem, dt.dcnt))
                self.waited[eng][key] = dt.dcnt

    def finish(self):
        for dt in self.dma_tiles:
            self.q["sp"].append(("w", dt.dsem, dt.dcnt))
        for e in self.ENG:
            if e != "sp" and self.cnt[e] > 0:
                self.q["sp"].append(("w", self.sem[e], self.cnt[e]))

    def emit(self, block):
        def replay(eng):
            def body(e):
                for it in self.q[eng]:
                    if it[0] == "w":
                        e.wait_ge(it[1], it[2])
                    elif it[0] == "i":
                        it[1](e).then_inc(it[2], 1)
                    elif it[0] == "d":
                        it[1](e).then_inc(it[2], 16)
                    else:
                        it[1](e).then_inc(it[2], it[3])
            return body
        block.tensor(replay("pe"))
        block.scalar(replay("act"))
        block.vector(replay("dve"))
        block.gpsimd(replay("pool"))
        block.sync(replay("sp"))


class Opnd:
    def __init__(self, t, ap):
        self.t = t
        self.ap = ap


def _o(trk, idx=None):
    return Opnd(trk, trk.ap[idx] if idx is not None else trk.ap)


Trk.__call__ = lambda self, *idx: Opnd(self, self.ap[idx if len(idx) != 1 else idx[0]])


def _rw(*ops):
    return [o.t for o in ops if isinstance(o, Opnd)]


def _a(x):
    return x.ap if isinstance(x, Opnd) else x


class Ops:
    def __init__(self, S):
        self.S = S

    def tt(self, eng, out, a, b, op):
        self.S.op(eng, lambda e: e.tensor_tensor(out=out.ap, in0=a.ap, in1=b.ap, op=op), reads=_rw(a, b), writes=[out.t])

    def ts(self, eng, out, a, s1, op0, s2=None, op1=None):
        if op1 is None:
            self.S.op(eng, lambda e: e.tensor_scalar(out=out.ap, in0=a.ap, scalar1=_a(s1), scalar2=None, op0=op0),
                      reads=_rw(a, s1), writes=[out.t])
        else:
            self.S.op(eng, lambda e: e.tensor_scalar(out=out.ap, in0=a.ap, scalar1=_a(s1), scalar2=_a(s2), op0=op0, op1=op1),
                      reads=_rw(a, s1, s2), writes=[out.t])

    def stt(self, out, a, s, b, op0, op1):
        self.S.op("dve", lambda e: e.scalar_tensor_tensor(out=out.ap, in0=a.ap, scalar=_a(s), in1=b.ap, op0=op0, op1=op1),
                  reads=_rw(a, s, b), writes=[out.t])

    def act(self, out, a, func, scale=None, bias=None):
        kw = {}
        if scale is not None:
            kw["scale"] = _a(scale)
        if bias is not None:
            kw["bias"] = _a(bias)
        self.S.op("act", lambda e: e.activation(out=out.ap, in_=a.ap, func=func, **kw), reads=_rw(a, scale, bias), writes=[out.t])

    def copy(self, eng, out, a):
        self.S.op(eng, lambda e: e.tensor_copy(out=out.ap, in_=a.ap), reads=[a.t], writes=[out.t])

    def scan(self, out, d0, d1, init, op0=ALU.mult, op1=ALU.add):
        self.S.op("dve", lambda e: e.tensor_tensor_scan(out=out.ap, data0=d0.ap, data1=d1.ap, initial=_a(init), op0=op0, op1=op1),
                  reads=_rw(d0, d1, init), writes=[out.t])

    def mm(self, out, lhsT, rhs, start=True, stop=True, skip=False):
        self.S.op("pe", lambda e: e.matmul(out.ap, lhsT=lhsT.ap, rhs=rhs.ap, start=start, stop=stop, skip_group_check=skip),
                  reads=[lhsT.t, rhs.t] + ([] if start else [out.t]), writes=[out.t])

    def memset(self, eng, out, val):
        self.S.op(eng, lambda e: e.memset(out.ap, val), writes=[out.t])

    def ld(self, out, src, eng="sp"):
        self.S.dma(eng, out.ap, src, out.t, writes=[out.t])

    def st(self, dst, src, eng="pool"):
        self.S.dma(eng, dst, src.ap, src.t, reads=[src.t])

PI = float(np.pi)


class _Cut(Exception):
    pass


def cut(name):
    if KSTAGE == name:
        raise _Cut()


def build(T=T_FULL, P=P_FULL):
    nc = bass.Bass("TRN2", target_bir_lowering=False)
    S = Sched(nc)
    O = Ops(S)
    es = contextlib.ExitStack()
    NS = NSB * DS
    A_ = slice(None)

    def din(name, shape, dt=F32):
        return nc.dram_tensor(name, list(shape), dt, kind="ExternalInput").ap()

    def dout(name, shape, dt=F32):
        return nc.dram_tensor(name, list(shape), dt, kind="ExternalOutput").ap()

    scopes = [es]

    uid = [0]

    def sb(name, shape, dt=F32):
        uid[0] += 1
        t = scopes[-1].enter_context(nc.sbuf_tensor(f"{name}_u{uid[0]}", list(shape), dt))
        return Trk(t, name)

    def tmp_open():
        scopes.append(contextlib.ExitStack())

    def tmp_close():
        S.barrier()
        scopes.pop().close()

    def ps(name, shape, dt=F32):
        t = es.enter_context(nc.psum_tensor(name, list(shape), dt))
        return Trk(t, name)

    xT_p = din("xT_p", [D, T])
    xT_s = din("xT_s", [D, NS])
    g_mix0 = din("g_mix0", [128, KC])
    w_in_a = din("w_in_a", [D, 512])
    s5col = din("s5col", [128, 3, 4])
    s5row = din("s5row", [128, 3, 512])
    s5B = din("s5B", [128, 2, 512])
    s5C = din("s5C", [128, 2, 512])
    s5d = din("s5d", [128, 1])
    s5h0 = din("s5h0", [128, 2, NSB, 4])
    cmask = din("cmask", [128, 2, 896])
    ctri = din("ctri", [128, 4, 128])
    sbkc = din("sbkc", [128, NSB, P])
    sbvc = din("sbvc", [P, NSB, 128])
    TP = T + NS
    DFF = 2816
    NF = DFF // 128
    NFL = 6
    WOUT = [din(f"w_out{l}", [D, D]) for l in range(2)]
    w_glu = din("w_glu", [512, 512])
    b_glu = din("b_glu", [128, 4])
    GFF = [din(f"g_ffn{l}", [128, KC]) for l in range(2)]
    WUP = [din(f"w_up{l}", [D, NFL, 256]) for l in range(2)]
    WDN = [din(f"w_dn{l}", [NFL * 128, D]) for l in range(2)]
    CW = [din(f"cw{l}", [128, NFL, 4]) for l in range(2)]
    CST = [din(f"cst{l}", [128, NSB, NFL, 2]) for l in range(2)]
    g_fin = din("g_fin", [128, KC])
    g_mix1 = din("g_mix1", [128, KC])
    w_in_b = din("w_in_b", [D, 1024])
    hgp = din("hgp", [128, 3])
    bfrow = din("bfrow", [128, 8])
    hgS0 = din("hgS0", [128, NSB, 128])
    fkc = din("fkc", [128, NSB, P])
    fvc = din("fvc", [P, NSB, 128])
    flc = din("flc", [128, NSB, P // 128, 2])
    o_fkT = dout("o_fkT", [128, T])
    o_fv = dout("o_fv", [T, 128])
    o_flf = dout("o_flf", [T, 2])
    o_fkT_s = dout("o_fkT_s", [128, NS])
    o_fv_s = dout("o_fv_s", [NS, 128])
    o_flf_s = dout("o_flf_s", [NS, 2])
    o_hg = dout("o_hg", [128, (1 + NSB) * 128])
    o_y = dout("o_y", [D, TP])
    XS = nc.dram_tensor("XS", [D, TP], F32).ap()
    OCONV = [dout(f"o_conv{l}", [128, (1 + NSB) * NFL * 2]) for l in range(2)]
    NTL = T // NT + 1
    tile_n = [NT] * (T // NT) + [NS]
    mixg_t = Trk(None, "mixg")
    mixg_t.dsem = S._alloc("cc")
    S.dma_tiles.append(mixg_t)

    def mk_exch(tag):
        MB = [nc.dram_tensor(f"MIXB{tag}_{i}", [256, tile_n[i]], BF16).ap() for i in range(NTL)]
        MG = [nc.dram_tensor(f"MIXG{tag}_{i}", [1024, tile_n[i]], BF16).ap() for i in range(NTL)]
        return MB, MG

    def exchange(MBi, MGi, srcs):
        S.wait_tiles("pool", list(srcs))
        S.raw("pool", lambda e: e.collective_compute("AllGather", ALU.bypass, replica_groups=[[0, 1, 2, 3], [4, 5, 6, 7]],
                                                      ins=[MBi.opt()], outs=[MGi.opt()]),
              mixg_t.dsem, 1, dt=mixg_t)
        mixg_t.w = ("d", mixg_t, None, "pool")

    MB0, MG0 = mk_exch("a")
    o_kT = dout("o_kT", [128, T])
    o_v = dout("o_v", [T, 128])
    o_kT_s = dout("o_kT_s", [128, NS])
    o_v_s = dout("o_v_s", [NS, 128])
    o_s5 = dout("o_s5", [128, (1 + NSB) * 8])

    ones_bf = sb("ones_bf", [128, 128], BF16)
    O.memset("pool", ones_bf(), 1.0)
    onesf = sb("onesf", [128, NT])
    O.memset("pool", onesf(), 1.0)
    gm0 = sb("gm0", [128, KC])
    O.ld(gm0(), g_mix0[:, :])
    gfin = sb("gfin", [128, KC])
    O.ld(gfin(), g_fin[:, :])
    masks_bf = sb("masks_bf", [128, 2, 896], BF16)
    tri_bf = sb("tri_bf", [128, 4, 128], BF16)
    trif = sb("trif", [128, 128])
    bank = [ps(f"bank{i}", [128, NT]) for i in range(8)]
    tmp_open()
    wa = sb("wa", [128, KC, 512], BF16)
    LT = NT
    cosT = [sb(f"cosT{j}", [128, LT]) for j in range(4)]
    sinT = [sb(f"sinT{j}", [128, LT]) for j in range(4)]
    rhoT = [sb(f"rhoT{j}", [128, LT]) for j in range(4)]
    Bbre = sb("Bbre", [128, 512], BF16)
    Bbim = sb("Bbim", [128, 512], BF16)
    Cre = sb("Cre", [128, 512], BF16)
    nCim = sb("nCim", [128, 512], BF16)
    dcol = sb("s5dcol", [128, 1])
    z0 = sb("z0", [128, (1 + NSB) * 8])
    k_all = sb("k_all", [128, T], BF16)
    v_all = sb("v_all", [128, T // 128, 128], BF16)
    tmp_open()
    wst = sb("wst", [128, KC, 512])
    mst = sb("mst", [128, 2, 896])
    O.ld(mst(), cmask[:, :, :])
    O.copy("pool", masks_bf(), mst())
    tst = sb("tst", [128, 4, 128])
    O.ld(tst(), ctri[:, :, :])
    O.copy("pool", tri_bf(), tst())
    O.copy("pool", trif(), tst(A_, 2, A_))
    O.ld(wst(), w_in_a.rearrange("(kc p) n -> p kc n", p=128))
    for kc in range(KC):
        O.copy("pool" if kc % 2 else "dve", wa(slice(None), kc, slice(None)), wst(slice(None), kc, slice(None)))


    try:
        pc = sb("s5pc", [128, 3, 4])
        O.ld(pc(), s5col[:, :, :])
        pr = sb("s5pr", [128, 3, 512])
        O.ld(pr(), s5row[:, :, :])
        Bst = sb("s5Bst", [128, 2, 512])
        O.ld(Bst(), s5B[:, :, :])
        Cst = sb("s5Cst", [128, 2, 512])
        O.ld(Cst(), s5C[:, :, :])
        O.ld(dcol(), s5d[:, :])
        h0 = sb("s5h0t", [128, 2, NSB, 4])
        O.ld(h0(), s5h0[:, :, :, :])
        cut('c_loads')

        def trig(theta, n, name):
            k = sb(name + "_k", [128, n])
            tmp = sb(name + "_t", [128, n])
            red = sb(name + "_r", [128, n])
            O.ts("dve", k(), theta, PI, ALU.is_gt)
            for m in range(2, 8):
                O.ts("dve", tmp(), theta, (2 * m - 1) * PI, ALU.is_gt)
                O.tt("dve", k(), k(), tmp(), ALU.add)
            O.stt(red(), k(), -2 * PI, theta, ALU.mult, ALU.add)
            sn = sb(name + "_sin", [128, n])
            cs = sb(name + "_cos", [128, n])
            O.act(sn(), red(), AF.Sin)
            O.ts("dve", tmp(), red(), PI / 2, ALU.is_gt)
            O.ts("dve", tmp(), tmp(), -2 * PI, ALU.mult, PI / 2, ALU.add)
            O.tt("dve", tmp(), tmp(), red(), ALU.add)
            O.act(cs(), tmp(), AF.Sin)
            return cs, sn

        dtc = sb("dtc", [128, 4])
        O.act(dtc(), pc(A_, 2, A_), AF.Exp)
        rhoc = sb("rhoc", [128, 4])
        O.tt("dve", rhoc(), pc(A_, 0, A_), dtc(), ALU.mult)
        O.act(rhoc(), rhoc(), AF.Exp)
        thc = sb("thc", [128, 4])
        O.tt("dve", thc(), pc(A_, 1, A_), dtc(), ALU.mult)
        cosc, sinc = trig(thc(), 4, "tc")
        cut('c_trig')
        nsm = sb("nsm", [128, 1])
        for j in range(4):
            O.ts("dve", rhoT[j](), onesf(A_, slice(0, LT)), rhoc(A_, slice(j, j + 1)), ALU.mult)
            O.copy("dve", cosT[j](A_, slice(0, 1)), cosc(A_, slice(j, j + 1)))
            O.copy("dve", sinT[j](A_, slice(0, 1)), sinc(A_, slice(j, j + 1)))
            m = 1
            while m < LT:
                cm = cosT[j](A_, slice(m - 1, m))
                sm = sinT[j](A_, slice(m - 1, m))
                O.ts("dve", nsm(), sm, -1.0, ALU.mult)
                lo, hi = slice(0, m), slice(m, 2 * m)
                O.ts("dve", cosT[j](A_, hi), cosT[j](A_, lo), cm, ALU.mult)
                O.stt(cosT[j](A_, hi), sinT[j](A_, lo), nsm(), cosT[j](A_, hi), ALU.mult, ALU.add)
                O.ts("dve", sinT[j](A_, hi), cosT[j](A_, lo), sm, ALU.mult)
                O.stt(sinT[j](A_, hi), sinT[j](A_, lo), cm, sinT[j](A_, hi), ALU.mult, ALU.add)
                m *= 2

        cut('c_tab')
        lr, li = pr(A_, 0, A_), pr(A_, 1, A_)
        dtr = sb("dtr", [128, 512])
        O.act(dtr(), pr(A_, 2, A_), AF.Exp)
        rr = sb("rr", [128, 512])
        O.tt("dve", rr(), lr, dtr(), ALU.mult)
        O.act(rr(), rr(), AF.Exp)
        thr = sb("thr", [128, 512])
        O.tt("dve", thr(), li, dtr(), ALU.mult)
        cosr, sinr = trig(thr(), 512, "tr")
        nre = sb("nre", [128, 512])
        nim = sb("nim", [128, 512])
        O.tt("dve", nre(), rr(), cosr(), ALU.mult)
        O.ts("dve", nre(), nre(), -1.0, ALU.add)
        O.tt("dve", nim(), rr(), sinr(), ALU.mult)
        den = sb("den", [128, 512])
        w1 = sb("w1", [128, 512])
        w2 = sb("w2", [128, 512])
        O.tt("dve", den(), lr, lr, ALU.mult)
        O.tt("dve", w1(), li, li, ALU.mult)
        O.tt("dve", den(), den(), w1(), ALU.add)
        S.op("dve", lambda e: e.reciprocal(out=den[:, :], in_=den[:, :]), reads=[den], writes=[den])
        kre = sb("kre", [128, 512])
        kim = sb("kim", [128, 512])
        O.tt("dve", kre(), nre(), lr, ALU.mult)
        O.tt("dve", w1(), nim(), li, ALU.mult)
        O.tt("dve", kre(), kre(), w1(), ALU.add)
        O.tt("dve", kre(), kre(), den(), ALU.mult)
        O.tt("dve", kim(), nim(), lr, ALU.mult)
        O.tt("dve", w1(), nre(), li, ALU.mult)
        O.tt("dve", kim(), kim(), w1(), ALU.subtract)
        O.tt("dve", kim(), kim(), den(), ALU.mult)
        O.tt("dve", w1(), kre(), Bst(A_, 0, A_), ALU.mult)
        O.tt("dve", w2(), kim(), Bst(A_, 1, A_), ALU.mult)
        O.tt("dve", Bbre(), w1(), w2(), ALU.subtract)
        O.tt("dve", w1(), kre(), Bst(A_, 1, A_), ALU.mult)
        O.tt("dve", w2(), kim(), Bst(A_, 0, A_), ALU.mult)
        O.tt("dve", Bbim(), w1(), w2(), ALU.add)
        O.copy("dve", Cre(), Cst(A_, 0, A_))
        O.ts("dve", nCim(), Cst(A_, 1, A_), -1.0, ALU.mult)

        cut('c_kap')
        O.memset("dve", z0(A_, slice(0, 8)), 0.0)
        for sb_ in range(NSB):
            for ri in range(2):
                O.copy("dve", z0(A_, slice(8 * (1 + sb_) + 4 * ri, 8 * (1 + sb_) + 4 * ri + 4)), h0(A_, ri, sb_, A_))

        cut('c_z0')
        tmp_close()
        cut('c_t0')
        NB = 2
        xt = [sb(f"xt{i}", [128, KC, NT]) for i in range(1)]
        sq = [sb(f"sq{i}", [128, KC, NT], BF16) for i in range(1)]
        hT = [sb(f"hT{i}", [128, KC, NT], BF16) for i in range(1)]
        lnv = [sb(f"lnv{i}", [128, NT]) for i in range(1)]
        rstd = [sb(f"rstd{i}", [128, NT]) for i in range(1)]
        kst = [sb(f"kst{i}", [128, NT]) for i in range(NB)]
        vst = [sb(f"vst{i}", [128, 4, 128]) for i in range(NB)]
        uf = [sb(f"uf{i}", [128, NT]) for i in range(1)]
        ub = [sb(f"ub{i}", [128, NT], BF16) for i in range(1)]
        s5o = [sb(f"s5o{i}", [128, NT], BF16) for i in range(NB)]
        tmpf = [sb(f"tmpf{i}", [128, NT]) for i in range(4)]
        vre, vim, zre, zim = (sb(n, [128, NT]) for n in ("vre", "vim", "zre", "zim"))
        hre = sb("hre", [128, NT], BF16)
        him = sb("him", [128, NT], BF16)
        ysum = vre
        ctmp = sb("ctmp", [128, 4])
        qT = sb("qT", [128, NT], BF16)
        e_ = [[sb(f"e_{h}{p}", [128, NT], BF16) for p in range(2)] for h in range(2)]
        sp_ = [[sb(f"sp_{h}{p}", [128, NT], BF16) for p in range(3)] for h in range(2)]
        x_ = [sb(f"x_{h}", [128, NT], BF16) for h in range(2)]
        w_ = [[sb(f"w_{h}{p}", [128, NT], BF16) for p in range(2)] for h in range(2)]
        mixsb = [[sb(f"mixsb{h}{p}", [64, NT], BF16) for p in range(1)] for h in range(2)]
        knew = [sb(f"knew{i}", [128, 128], BF16) for i in range(NSB)]
        vnew = [sb(f"vnew{i}", [128, 128], BF16) for i in range(NSB)]
        kcs = [sb(f"kcs{i}", [128, 512]) for i in range(1)] * 2
        kcb = [sb(f"kcb{i}", [128, 512], BF16) for i in range(2)]
        vcs = [sb(f"vcs{i}", [128, 4, 128]) for i in range(1)] * 2
        vcb = [sb(f"vcb{i}", [128, 4, 128], BF16) for i in range(2)]
        zb_double, Pb, Ob = [[bank[1], bank[7]], [bank[2], bank[0]]], [bank[3], bank[4]], [bank[5], bank[6]]
        zb_single = [[bank[1], bank[1]], [bank[2], bank[2]]]
        triI = tri_bf(A_, 0, A_)
        triC = tri_bf(A_, 1, A_)

        def sb_attend(qfn, n, blocks, outs, oc0, side=None):
            ns = slice(0, n)
            nb = len(blocks)
            info = {}
            zb = zb_double if side is None else zb_single
            pulls = 1 if nb >= 48 else -(-48 // nb)

            def stA(i):
                info[i] = blocks[i]()
                kf = info[i][0]
                for hd in range(2):
                    O.mm(zb[hd][i % 2](A_, ns), kf(hd), qfn(hd))

            def stB(i):
                mk = info[i][2]
                for hd in range(2):
                    E = e_[hd][i % 2]
                    O.act(E(A_, ns), zb[hd][i % 2](A_, ns), AF.Exp, scale=0.125)
                    if mk is not None:
                        O.tt("pool", E(A_, ns), E(A_, ns), mk, ALU.mult)
                for hd in range(2):
                    O.act(sp_[hd][i % 3](A_, ns), e_[hd][i % 2](A_, ns), AF.Ln, bias=1.0)

            def stC(i):
                for hd in range(2):
                    if i > 0:
                        O.mm(Pb[hd](A_, ns), triC, sp_[hd][(i - 1) % 3](A_, ns), start=False, stop=False, skip=True)
                    O.mm(Pb[hd](A_, ns), triI, sp_[hd][i % 3](A_, ns), start=(i == 0), stop=True, skip=True)

            def stD(i):
                for hd in range(2):
                    O.act(x_[hd](A_, ns), Pb[hd](A_, ns), AF.Exp, scale=-1.0)
                for hd in range(2):
                    O.tt("dve", w_[hd][i % 2](A_, ns), e_[hd][i % 2](A_, ns), x_[hd](A_, ns), ALU.mult)

            def stE(i):
                vf = info[i][1]
                for hd in range(2):
                    O.mm(Ob[hd](slice(0, 64), ns), vf(hd), w_[hd][i % 2](A_, ns), start=(i == 0), stop=(i == nb - 1))

            stA(0)
            stB(0)
            if nb > 1:
                stA(1)
            for i in range(nb):
                stC(i)
                if i + 1 < nb:
                    stB(i + 1)
                stD(i)
                if i + 2 < nb:
                    stA(i + 2)
                stE(i)
                if side is not None:
                    for _ in range(pulls):
                        next(side, None)
            if side is not None:
                for _ in side:
                    pass
            for hd in range(2):
                O.act(outs[hd](A_, slice(oc0, oc0 + n)), Ob[hd](slice(0, 64), ns), AF.Copy)

        def rms_tile(src_ap, n, gcol, it):
            b = it % NB
            X, Q, H, L, R = xt[0], sq[0], hT[0], lnv[0], rstd[0]
            S.dma("sp", X[:, :, :n], src_ap.rearrange("(kc p) n -> p kc n", p=128), X, writes=[X])
            O.act(Q(A_, A_, slice(0, n)), X(A_, A_, slice(0, n)), AF.Square)
            ssb = bank[0]
            for kc in range(KC):
                O.mm(ssb(A_, slice(0, n)), ones_bf(), Q(A_, kc, slice(0, n)), start=(kc == 0), stop=(kc == KC - 1))
            O.act(L(A_, slice(0, n)), ssb(A_, slice(0, n)), AF.Ln, scale=1.0 / D, bias=EPS)
            O.act(R(A_, slice(0, n)), L(A_, slice(0, n)), AF.Exp, scale=-0.5)
            for kc in range(KC):
                O.stt(H(A_, kc, slice(0, n)), X(A_, kc, slice(0, n)), gcol(A_, slice(kc, kc + 1)), R(A_, slice(0, n)),
                      ALU.mult, ALU.mult)
            return X, H

        def proj_fm(H, n, wt, c0, pbank, m=128):
            for kc in range(KC):
                O.mm(pbank(slice(0, m), slice(0, n)), wt(A_, kc, slice(c0, c0 + m)), H(A_, kc, slice(0, n)),
                     start=(kc == 0), stop=(kc == KC - 1))

        def proj_tm(H, n, wt, c0, pbank):
            nsub = (n + 127) // 128
            for s_ in range(nsub):
                m = min(128, n - s_ * 128)
                for kc in range(KC):
                    O.mm(pbank(slice(0, m), slice(s_ * 128, (s_ + 1) * 128)), H(A_, kc, slice(s_ * 128, s_ * 128 + m)),
                         wt(A_, kc, slice(c0, c0 + 128)), start=(kc == 0), stop=(kc == KC - 1))

        def s5_gen(UF, UB, c0, L, seq, OUT, bu, yb):
            cs = slice(c0, c0 + L)
            ls = slice(0, L)
            for j in range(4):
                js = slice(128 * j, 128 * j + 128)
                c_, s_ = cosT[j](A_, ls), sinT[j](A_, ls)
                O.mm(bu(A_, ls), Bbre(A_, js), UB(A_, cs))
                O.tt("dve", tmpf[0](A_, ls), c_, bu(A_, ls), ALU.mult)
                O.tt("dve", tmpf[3](A_, ls), s_, bu(A_, ls), ALU.mult)
                yield
                O.mm(bu(A_, ls), Bbim(A_, js), UB(A_, cs))
                O.tt("dve", tmpf[1](A_, ls), s_, bu(A_, ls), ALU.mult)
                O.tt("dve", tmpf[2](A_, ls), c_, bu(A_, ls), ALU.mult)
                yield
                O.tt("pool", vre(A_, ls), tmpf[0](A_, ls), tmpf[1](A_, ls), ALU.add)
                O.tt("pool", vim(A_, ls), tmpf[2](A_, ls), tmpf[3](A_, ls), ALU.subtract)
                yield
                O.scan(zre(A_, ls), rhoT[j](A_, ls), vre(A_, ls), z0(A_, slice(8 * seq + j, 8 * seq + j + 1)))
                yield
                O.scan(zim(A_, ls), rhoT[j](A_, ls), vim(A_, ls), z0(A_, slice(8 * seq + 4 + j, 8 * seq + 4 + j + 1)))
                yield
                O.tt("pool", tmpf[0](A_, ls), c_, zre(A_, ls), ALU.mult)
                O.tt("pool", tmpf[1](A_, ls), s_, zim(A_, ls), ALU.mult)
                yield
                O.tt("pool", hre(A_, ls), tmpf[0](A_, ls), tmpf[1](A_, ls), ALU.subtract)
                O.tt("pool", tmpf[2](A_, ls), s_, zre(A_, ls), ALU.mult)
                yield
                O.tt("pool", tmpf[3](A_, ls), c_, zim(A_, ls), ALU.mult)
                O.tt("pool", him(A_, ls), tmpf[2](A_, ls), tmpf[3](A_, ls), ALU.add)
                yield
                e_ = slice(L - 1, L)
                O.tt("dve", z0(A_, slice(8 * seq + j, 8 * seq + j + 1)), tmpf[0](A_, e_), tmpf[1](A_, e_), ALU.subtract)
                O.tt("dve", z0(A_, slice(8 * seq + 4 + j, 8 * seq + 4 + j + 1)), tmpf[2](A_, e_), tmpf[3](A_, e_), ALU.add)
                O.mm(yb(A_, ls), Cre(A_, js), hre(A_, ls), start=(j == 0), stop=False, skip=True)
                O.mm(yb(A_, ls), nCim(A_, js), him(A_, ls), start=False, stop=(j == 3), skip=True)
                yield
            O.stt(ysum(A_, ls), UF(A_, cs), dcol(A_, slice(0, 1)), yb(A_, ls), ALU.mult, ALU.add)
            O.tt("pool", tmpf[0](A_, ls), ysum(A_, ls), ysum(A_, ls), ALU.mult)
            yield
            O.ts("pool", tmpf[0](A_, ls), tmpf[0](A_, ls), 0.044715, ALU.mult, 1.0, ALU.add)
            O.tt("pool", tmpf[0](A_, ls), tmpf[0](A_, ls), ysum(A_, ls), ALU.mult)
            yield
            O.act(tmpf[1](A_, ls), tmpf[0](A_, ls), AF.Sigmoid, scale=1.5957691216057308)
            O.tt("dve", OUT(A_, cs), ysum(A_, ls), tmpf[1](A_, ls), ALU.mult)

        def s5_chunk(UF, UB, c0, L, seq, OUT):
            for _ in s5_gen(UF, UB, c0, L, seq, OUT, bank[5], bank[7]):
                pass

        def layer0_tile(src_ap, n, it, okT_ap, ov_ap, seqs, mc0, prompt_j):
            b = it % NB
            X, H = rms_tile(src_ap, n, gm0, it)
            kb = bank[1 + (it % 2)]
            proj_fm(H, n, wa, 256, kb)
            K = kst[b]
            O.act(K(A_, slice(0, n)), kb(A_, slice(0, n)), AF.Copy)
            O.st(okT_ap, K(A_, slice(0, n)))
            vb = bank[3]
            proj_tm(H, n, wa, 384, vb)
            V = vst[b]
            nsub = (n + 127) // 128
            m = min(128, n)
            S.op("dve", lambda e: e.tensor_copy(out=V[:m, :nsub, :], in_=vb[:m, :nsub * 128].rearrange("p (s c) -> p s c", c=128)),
                 reads=[vb], writes=[V])
            if n >= 128:
                S.dma("pool", ov_ap.rearrange("(s p) c -> p s c", p=128), V[:, :nsub, :], V, reads=[V])
            else:
                S.dma("pool", ov_ap, V[:n, 0, :], V, reads=[V])
            if KSTAGE == 'nou':
                return
            ubk = bank[4]
            proj_fm(H, n, wa, 0, ubk)
            O.act(uf[0](A_, slice(0, n)), ubk(A_, slice(0, n)), AF.Copy)
            O.copy("dve", ub[0](A_, slice(0, n)), uf[0](A_, slice(0, n)))
            MBi = MB0[it]
            side = None
            if prompt_j is not None:
                side = s5_gen(uf[0], ub[0], 0, NT, 0, s5o[b], bank[7], bank[0])
            else:
                for (c0, L, seq) in seqs:
                    s5_chunk(uf[0], ub[0], c0, L, seq, s5o[b])
                S.dma("pool", MBi[0:128, 0:n], s5o[b][:, :n], s5o[b], reads=[s5o[b]])
            cut('c_s5only')
            proj_fm(H, n, wa, 128, bank[4])
            O.act(qT(A_, slice(0, n)), bank[4](A_, slice(0, n)), AF.Copy)
            ms = [mixsb[0][0], mixsb[1][0]]
            if prompt_j is not None:
                j = prompt_j
                O.copy("pool", k_all(A_, slice(j * NT, j * NT + n)), K(A_, slice(0, n)))
                O.copy("pool", v_all(A_, slice(4 * j, 4 * j + 4), A_), V(A_, slice(0, 4), A_))
                blocks = []
                for kb_ in range(4 * j + 3, -1, -1):
                    m_ = kb_ - 4 * j
                    mk = masks_bf(A_, 0, slice(384 - 128 * m_, 896 - 128 * m_)) if m_ >= 0 else None
                    blocks.append(lambda kb_=kb_, mk=mk: (
                        lambda hd: k_all(slice(64 * hd, 64 * hd + 64), slice(kb_ * 128, kb_ * 128 + 128)),
                        lambda hd: v_all(A_, kb_, slice(64 * hd, 64 * hd + 64)), mk))
                sb_attend(lambda hd: qT(slice(64 * hd, 64 * hd + 64), slice(0, n)), n, blocks, ms, 0, side=side)
                S.dma("pool", MBi[0:128, 0:n], s5o[b][:, :n], s5o[b], reads=[s5o[b]])
            else:
                for sb_ in range(NSB):
                    c0 = DS * sb_
                    O.memset("pool", knew[sb_](), 0.0)
                    O.memset("pool", vnew[sb_](), 0.0)
                    O.copy("pool", knew[sb_](A_, slice(0, DS)), K(A_, slice(c0, c0 + DS)))
                    for kc in range(KC):
                        O.mm(bank[7](slice(0, DS), slice(0, 128)), H(A_, kc, slice(c0, c0 + DS)), wa(A_, kc, slice(384, 512)),
                             start=(kc == 0), stop=(kc == KC - 1))
                    O.copy("dve", vnew[sb_](slice(0, DS), A_), bank[7](slice(0, DS), slice(0, 128)))
                    blocks = [lambda sb_=sb_: (lambda hd: knew[sb_](slice(64 * hd, 64 * hd + 64), A_),
                                               lambda hd: vnew[sb_](A_, slice(64 * hd, 64 * hd + 64)),
                                               masks_bf(A_, 0, slice(384, 384 + DS)))]
                    for kb_ in range(P // 128 - 1, -1, -1):
                        def blk(kb_=kb_, sb_=sb_):
                            ci, sub = kb_ // 4, kb_ % 4
                            pr_ = ci % 2
                            if sub == 3:
                                O.ld(kcs[pr_](), sbkc[:, sb_, ci * 512:(ci + 1) * 512])
                                O.copy("pool", kcb[pr_](), kcs[pr_]())
                                O.ld(vcs[pr_](), sbvc[ci * 512:(ci + 1) * 512, sb_, :].rearrange("(s p) c -> p s c", p=128))
                                O.copy("pool", vcb[pr_](), vcs[pr_]())
                            return (lambda hd: kcb[pr_](slice(64 * hd, 64 * hd + 64), slice(sub * 128, sub * 128 + 128)),
                                    lambda hd: vcb[pr_](A_, sub, slice(64 * hd, 64 * hd + 64)), None)
                        blocks.append(blk)
                    sb_attend(lambda hd, c0=c0: qT(slice(64 * hd, 64 * hd + 64), slice(c0, c0 + DS)), DS, blocks, ms, c0)
            for hd in range(2):
                S.dma("pool", MBi[128 + 64 * hd:192 + 64 * hd, 0:n], ms[hd][:, :n], ms[hd], reads=[ms[hd]])
            exchange(MBi, MG0[it], [s5o[b], ms[0], ms[1]])

        it = 0
        for j in range(T // NT):
            layer0_tile(xT_p[:, j * NT:(j + 1) * NT], NT, it, o_kT[:, j * NT:(j + 1) * NT], o_v[j * NT:(j + 1) * NT, :],
                        [(0, NT, 0)], j * NT, j)
            it += 1
            cut('c_t1')
        layer0_tile(xT_s[:, :], NS, it, o_kT_s[:, :], o_v_s[:, :], [(DS * i, DS, 1 + i) for i in range(NSB)], T, None)
        it += 1
        O.st(o_s5[:, :], z0(), eng='sp')
        cut('nocc')

        def run_phaseC(LY, MGL):
            tmp_open()
            woutb = sb("woutb", [128, KC, D], BF16)
            wglub = sb("wglub", [128, 4, 512], BF16)
            wdb = sb("wdb", [128, NFL, D], BF16)
            bgl = sb("bgl", [128, 4])
            gf0 = sb("gf0", [128, KC])
            cwt = sb("cwt", [128, NFL, 4])
            acar = sb("acar", [128, (1 + NSB) * NFL * 2])
            stg = sb("stg", [128, KC, 512])
            O.ld(bgl(), b_glu[:, :])
            O.ld(gf0(), GFF[LY][:, :])
            O.ld(cwt(), CW[LY][:, :, :])
            O.memset("dve", acar(A_, slice(0, NFL * 2)), 0.0)
            S.dma("sp", acar[:, NFL * 2:], CST[LY].rearrange("p s c k -> p (s c k)"), acar, writes=[acar])
            for hh in range(2):
                O.ld(stg(), WOUT[LY][:, hh * 512:(hh + 1) * 512].rearrange("(kc p) n -> p kc n", p=128))
                O.copy("pool" if hh else "dve", woutb(A_, A_, slice(hh * 512, hh * 512 + 512)), stg())
            if LY == 0:
                O.ld(stg(A_, slice(0, 4), A_), w_glu.rearrange("(kc p) n -> p kc n", p=128))
                O.copy("dve", wglub(), stg(A_, slice(0, 4), A_))
            for c in range(NFL):
                O.ld(stg(A_, slice(0, 2), A_), WDN[LY][c * 128:(c + 1) * 128, :].rearrange("p (a n) -> p a n", a=2))
                S.op("pool" if c % 2 else "dve",
                     lambda e, c=c: e.tensor_copy(out=wdb[:, c, :].rearrange("p (a n) -> p a n", a=2), in_=stg[:, 0:2, :]),
                     reads=[stg], writes=[wdb])
            mixt = sb("mixt", [128, KC, NT], BF16)
            Xb = [sb(f"Xc{i}", [128, KC, NT]) for i in range(2)]
            art = sb("art", [128, KC, NT])
            ARI = [nc.dram_tensor(f"ARI{LY}_{i}", [D, tile_n[i]], F32).ap() for i in range(NTL)]
            ARO = [nc.dram_tensor(f"ARO{LY}_{i}", [D, tile_n[i]], F32).ap() for i in range(NTL)]
            ar_t = Trk(None, "ar")
            ar_t.dsem = S._alloc("ar")
            S.dma_tiles.append(ar_t)
            ar_t.w = ("d", ar_t, None, "pool")
            pend = []
            Q = sb("Qc", [128, KC, NT], BF16)
            H = sb("Hc", [128, KC, NT], BF16)
            Lc = sb("Lc", [128, NT])
            Rc = sb("Rc", [128, NT])
            yg = sb("yg", [128, 4, NT], BF16)
            sig = sb("sig", [128, NT])
            aext = [sb(f"aext{i}", [128, NT + 2]) for i in range(2)]
            ct = [sb(f"ct{i}", [128, NT]) for i in range(2)]
            sg = [sb(f"sg{i}", [128, NT]) for i in range(2)]
            hid = sb("hid", [128, NFL, NT], BF16)
            wus = [sb(f"wus{i}", [128, KC, 256]) for i in range(2)]
            wub = [sb(f"wub{i}", [128, KC, 256], BF16) for i in range(3)]
            WUPB = nc.dram_tensor(f"WUPB{LY}", [NFL, 128, KC * 256], BF16).ap()
            wupb_t = Trk(None, "wupb")
            for c in range(NFL):
                p_ = c % 2
                O.ld(wus[p_](), WUP[LY][:, c, :].rearrange("(kc p) n -> p kc n", p=128))
                O.copy("pool" if c % 2 else "dve", wub[p_](), wus[p_]())
                S.dma("pool", WUPB[c].rearrange("p (kc n) -> p kc n", kc=KC), wub[p_][:, :, :], wupb_t,
                      reads=[wub[p_]], writes=[wupb_t])
            xo = [sb(f"xo{i}", [128, NT]) for i in range(2)]

            def phaseC_tile(src_ap, n, mc0, segs, MGsrc, ti):
                ns = slice(0, n)
                X = Xb[ti % 2]
                S.dma("sp", mixt[:, :, :n], MGsrc.rearrange("(kc p) n -> p kc n", p=128), mixt,
                      reads=[mixg_t], writes=[mixt])
                S.dma("sp", X[:, :, :n], src_ap.rearrange("(kc p) n -> p kc n", p=128), X, writes=[X])
                for oc in range(4 if LY == 0 else 0):
                    gb = bank[1 + oc % 2]
                    for ic in range(4):
                        O.mm(gb(A_, ns), wglub(A_, ic, slice(oc * 128, oc * 128 + 128)), mixt(A_, 2 * ic, ns),
                             start=(ic == 0), stop=(ic == 3))
                    O.act(sig(A_, ns), gb(A_, ns), AF.Sigmoid, bias=bgl(A_, slice(oc, oc + 1)))
                    O.tt("dve", yg(A_, oc, ns), mixt(A_, 2 * oc, ns), sig(A_, ns), ALU.mult)
                for oc in range(KC):
                    ob = bank[3 + oc % 2]
                    for kc in range(KC):
                        rhs = yg(A_, kc // 2, ns) if (kc % 2 == 0 and LY == 0) else mixt(A_, kc, ns)
                        O.mm(ob(A_, ns), woutb(A_, kc, slice(oc * 128, oc * 128 + 128)), rhs, start=(kc == 0), stop=(kc == KC - 1))
                    O.tt("dve", X(A_, oc, ns), ob(A_, ns), X(A_, oc, ns), ALU.add)
                O.act(Q(A_, A_, ns), X(A_, A_, ns), AF.Square)
                for kc in range(KC):
                    O.mm(bank[0](A_, ns), ones_bf(), Q(A_, kc, ns), start=(kc == 0), stop=(kc == KC - 1))
                O.act(Lc(A_, ns), bank[0](A_, ns), AF.Ln, scale=1.0 / D, bias=EPS)
                O.act(Rc(A_, ns), Lc(A_, ns), AF.Exp, scale=-0.5)
                for kc in range(KC):
                    O.stt(H(A_, kc, ns), X(A_, kc, ns), gf0(A_, slice(kc, kc + 1)), Rc(A_, ns), ALU.mult, ALU.mult)
                for c in range(NFL):
                    p_ = c % 2
                    w3 = c % 3
                    S.dma("sp", wub[w3][:, :, :], WUPB[c].rearrange("p (kc n) -> p kc n", kc=KC), wub[w3],
                          reads=[wupb_t], writes=[wub[w3]])
                    ab = bank[1 + p_]
                    for kc in range(KC):
                        O.mm(ab(A_, ns), wub[w3](A_, kc, slice(0, 128)), H(A_, kc, ns), start=(kc == 0), stop=(kc == KC - 1))
                    gbk = bank[5 + p_]
                    for kc in range(KC):
                        O.mm(gbk(A_, ns), wub[w3](A_, kc, slice(128, 256)), H(A_, kc, ns), start=(kc == 0), stop=(kc == KC - 1))
                    AE, CT, SG = aext[p_], ct[p_], sg[p_]
                    for (c0, L, seq) in segs:
                        cb = (seq * NFL + c) * 2
                        O.copy("pool", AE(A_, slice(0, 2)), acar(A_, slice(cb, cb + 2)))
                        O.act(AE(A_, slice(2, 2 + L)), ab(A_, slice(c0, c0 + L)), AF.Copy)
                        O.ts("dve", CT(A_, slice(0, L)), AE(A_, slice(2, 2 + L)), cwt(A_, c, slice(2, 3)), ALU.mult,
                             cwt(A_, c, slice(3, 4)), ALU.add)
                        O.stt(CT(A_, slice(0, L)), AE(A_, slice(1, 1 + L)), cwt(A_, c, slice(1, 2)), CT(A_, slice(0, L)), ALU.mult, ALU.add)
                        O.stt(CT(A_, slice(0, L)), AE(A_, slice(0, L)), cwt(A_, c, slice(0, 1)), CT(A_, slice(0, L)), ALU.mult, ALU.add)
                        O.copy("pool", acar(A_, slice(cb, cb + 2)), AE(A_, slice(L, L + 2)))
                        O.act(SG(A_, slice(0, L)), CT(A_, slice(0, L)), AF.Silu)
                        O.tt("dve", hid(A_, c, slice(c0, c0 + L)), SG(A_, slice(0, L)), gbk(A_, slice(c0, c0 + L)), ALU.mult)
                for oc in range(KC):
                    ob = bank[3 + oc % 2]
                    for c in range(NFL):
                        O.mm(ob(A_, ns), wdb(A_, c, slice(oc * 128, oc * 128 + 128)), hid(A_, c, ns), start=(c == 0), stop=(c == NFL - 1))
                    XO = xo[oc % 2]
                    O.act(XO(A_, ns), ob(A_, ns), AF.Copy)
                    S.dma("pool", ARI[ti][oc * 128:(oc + 1) * 128, 0:n], XO[:, :n], XO, reads=[XO])
                if pend:
                    epilogue(*pend.pop())
                S.wait_tiles("pool", [xo[0], xo[1]])
                S.raw("pool", lambda e: e.collective_compute("AllReduce", ALU.add, replica_groups=[[0, 1, 2, 3], [4, 5, 6, 7]],
                                                              ins=[ARI[ti].opt()], outs=[ARO[ti].opt()]),
                      ar_t.dsem, 1, dt=ar_t)
                ar_t.w = ("d", ar_t, None, "pool")
                pend.append((ti, n, mc0))

            def epilogue(ti, n, mc0):
                ns = slice(0, n)
                X = Xb[ti % 2]
                S.dma("sp", art[:, :, :n], ARO[ti].rearrange("(kc p) n -> p kc n", p=128), art, reads=[ar_t], writes=[art])
                O.tt("dve", X(A_, A_, ns), X(A_, A_, ns), art(A_, A_, ns), ALU.add)
                if LY == 0:
                    S.dma("pool", XS[:, mc0:mc0 + n].rearrange("(kc p) n -> p kc n", p=128), X[:, :, :n], X, reads=[X])
                else:
                    O.act(Q(A_, A_, ns), X(A_, A_, ns), AF.Square)
                    for kc in range(KC):
                        O.mm(bank[0](A_, ns), ones_bf(), Q(A_, kc, ns), start=(kc == 0), stop=(kc == KC - 1))
                    O.act(Lc(A_, ns), bank[0](A_, ns), AF.Ln, scale=1.0 / D, bias=EPS)
                    O.act(Rc(A_, ns), Lc(A_, ns), AF.Exp, scale=-0.5)
                    for kc in range(KC):
                        O.stt(art(A_, kc, ns), X(A_, kc, ns), gfin(A_, slice(kc, kc + 1)), Rc(A_, ns), ALU.mult, ALU.mult)
                    S.dma("pool", o_y[:, mc0:mc0 + n].rearrange("(kc p) n -> p kc n", p=128), art[:, :, :n], art, reads=[art])

            for j in range(T // NT):
                phaseC_tile((xT_p if LY == 0 else XS)[:, j * NT:(j + 1) * NT], NT, j * NT, [(0, NT, 0)], MGL[j], j)
            phaseC_tile(xT_s[:, :] if LY == 0 else XS[:, T:T + NS], NS, T, [(DS * i, DS, 1 + i) for i in range(NSB)], MGL[T // NT], T // NT)
            epilogue(*pend.pop())
            O.st(OCONV[LY][:, :], acar(), eng='sp')


            tmp_close()

        tmp_close()
        run_phaseC(0, MG0)
        if DBG_MIX:
            S.barrier()
            dbgx = dout("o_xs", [D, TP])
            dbx_t = Trk(None, "dbx")
            S.dma("pool", dbgx[:, :], XS[:, :], dbx_t)
        cut('l0only')
        S.barrier()

        MB1, MG1 = mk_exch("b")
        tmp_open()
        NKB = T // 128
        NCB = P // 128
        wb = sb("wb", [128, KC, 1024], BF16)
        gm1 = sb("gm1", [128, KC])
        O.ld(gm1(), g_mix1[:, :])
        k1_all = sb("k1_all", [128, T], BF16)
        v1_all = sb("v1_all", [128, NKB, 2, 65], BF16)
        O.memset("pool", v1_all(), 1.0)
        cKM = sb("cKM", [128, NKB, 2])
        cKMs = sb("cKMs", [128, NCB + 1, 2])
        btab = sb("btab", [128, max(NKB, NCB + 1), 2])
        carry = sb("carry", [128, 2])
        shq = sb("shq", [128, 2])
        O.memset("dve", carry(), 0.0)
        hgpt = sb("hgpt", [128, 3])
        O.ld(hgpt(), hgp[:, :])
        bfr = sb("bfr", [128, 8])
        O.ld(bfr(), bfrow[:, :])
        Sst = sb("Sst", [128, 1 + NSB, 128])
        O.memset("dve", Sst(A_, 0, A_), 0.0)
        S.dma("sp", Sst[:, 1:, :], hgS0[:, :, :], Sst, writes=[Sst])
        Sbf = sb("Sbf", [128, 128], BF16)
        lbe = sb("lbe", [128, 2])
        lbcol = sb("lbcol", [128, 1])
        omlb = sb("omlb", [128, 1])
        O.act(lbe(), hgpt(A_, slice(0, 2)), AF.Exp)
        O.tt("dve", lbcol(), lbe(A_, slice(0, 1)), lbe(A_, slice(1, 2)), ALU.add)
        S.op("dve", lambda e: e.reciprocal(out=lbcol[:, :], in_=lbcol[:, :]), reads=[lbcol], writes=[lbcol])
        O.tt("dve", lbcol(), lbcol(), lbe(A_, slice(1, 2)), ALU.mult)
        O.ts("dve", omlb(), lbcol(), -1.0, ALU.mult, 1.0, ALU.add)
        tmp_open()
        stg1 = sb("stg1", [128, KC, 512])
        for hh in range(2):
            O.ld(stg1(), w_in_b[:, hh * 512:(hh + 1) * 512].rearrange("(kc p) n -> p kc n", p=128))
            O.copy("pool" if hh else "dve", wb(A_, A_, slice(hh * 512, hh * 512 + 512)), stg1())
        tmp_close()
        X1 = sb("X1", [128, KC, NT])
        Q1 = sb("Q1", [128, KC, NT], BF16)
        H1 = sb("H1", [128, KC, NT], BF16)
        L1 = sb("L1", [128, NT])
        R1 = sb("R1", [128, NT])
        K1s = [sb(f"K1s{i}", [128, NT]) for i in range(2)]
        V1s = [sb(f"V1s{i}", [128, 4, 2, 64]) for i in range(2)]
        qT1 = sb("qT1", [128, NT], BF16)
        lft = sb("lft", [128, 8])
        lfo = [sb(f"lfo{i}", [128, 8]) for i in range(2)]
        pb_ = [[sb(f"pb_{h}{p}", [128, NT], BF16) for p in range(2)] for h in range(2)]
        oc_t = [sb(f"oc_t{h}", [128, NT]) for h in range(2)]
        rlt = [sb(f"rlt{h}", [128, NT]) for h in range(2)]
        mixfx = [[sb(f"mixfx{h}{p}", [64, NT], BF16) for p in range(2)] for h in range(2)]
        sigf, fg, kk, lgf, bcs, eb, enb, qs, exb, gs, onn = (sb(nm, [128, NT]) for nm in
                                                              ("sigf", "fg", "kk", "lgf", "bcs", "eb", "enb", "qs", "exb", "gs", "onn"))
        qt = sb("qt", [128, NT], BF16)
        kt = sb("kt", [128, NT], BF16)
        khT = sb("khT", [128, NT], BF16)
        osq = sb("osq", [128, NT], BF16)
        ebend = sb("ebend", [128, 8])
        vtm = sb("vtm", [128, 4, 128], BF16)
        vtms = [sb(f"vtms{i}", [DS, 128], BF16) for i in range(NSB)]
        scm = sb("scm", [128, 64], BF16)
        khtm = sb("khtm", [128, 128], BF16)
        hgo = [sb(f"hgo{i}", [128, NT], BF16) for i in range(2)]
        knew1 = [sb(f"knew1{i}", [128, 128], BF16) for i in range(NSB)]
        vnew1 = [sb(f"vnew1{i}", [128, 2, 65], BF16) for i in range(NSB)]
        lfn = [sb(f"lfn{i}", [DS, 2]) for i in range(NSB)]
        vfs = sb("vfs", [DS, 128])
        kcs1 = [sb(f"kcs1{i}", [128, 512]) for i in range(2)]
        kcb1 = [sb(f"kcb1{i}", [128, 512], BF16) for i in range(2)]
        vcs1 = [sb(f"vcs1{i}", [128, 4, 2, 64]) for i in range(2)]
        vcb1 = [sb(f"vcb1{i}", [128, 4, 2, 65], BF16) for i in range(2)]
        for i in range(2):
            O.memset("pool", vcb1[i](), 1.0)
        flct = sb("flct", [128, NCB, 2])
        identb = tri_bf(A_, 3, A_)
        gncol = hgpt(A_, slice(2, 3))

        def logf_block(lf_op, rows, dst, cr):
            cb_ = bank[7]
            O.mm(cb_(slice(0, rows), slice(0, 2)), trif(slice(0, rows), slice(0, rows)), lf_op)
            O.tt("dve", dst, cb_(slice(0, rows), slice(0, 2)), cr(slice(0, rows), A_), ALU.add)
            O.mm(cb_(A_, slice(8, 10)), onesf(slice(0, rows), slice(0, 128)), lf_op)
            O.tt("dve", cr(), cb_(A_, slice(8, 10)), cr(), ALU.add)

        def hg_gen(chunks, vfn, scb, ktb, kc0, snb, ob, sc0=0):
            cur_seq = None
            for ci, (c0, L, po, sbi, seq) in enumerate(chunks):
                if seq != cur_seq:
                    O.copy("pool", Sbf(), Sst(A_, seq, A_))
                    cur_seq = seq
                vop = vfn(c0, L, po, sbi)
                cs = slice(c0, c0 + L)
                pr_ = slice(po, po + L)
                O.mm(scb(pr_, slice(0, L)), kt(A_, cs), qt(A_, cs), skip=True)
                yield
                O.tt("dve", scm(pr_, slice(0, L)), scb(pr_, slice(0, L)), masks_bf(pr_, 1, slice(384 + po, 384 + po + L)), ALU.mult)
                yield
                O.mm(ob(A_, cs), vop, scm(pr_, slice(0, L)), start=True, stop=False, skip=True)
                O.mm(ob(A_, cs), Sbf(), qt(A_, cs), start=False, stop=True, skip=True)
                yield
                O.mm(ktb(pr_, slice(kc0, kc0 + 128)), khT(A_, cs), identb, skip=True)
                yield
                O.act(khtm(pr_, A_), ktb(pr_, slice(kc0, kc0 + 128)), AF.Copy)
                yield
                O.mm(snb(A_, slice(sc0, sc0 + 128)), khtm(pr_, A_), vop, skip=True)
                yield
                O.stt(Sst(A_, seq, A_), Sst(A_, seq, A_), ebend(A_, slice(ci, ci + 1)), snb(A_, slice(sc0, sc0 + 128)), ALU.mult, ALU.add)
                yield
                O.copy("pool", Sbf(), Sst(A_, seq, A_))
                yield

        def fox_attend(qfn, n, blocks, outs, oc0, side=None):
            ns = slice(0, n)
            zb1, Ob1, bcb = [[bank[1], bank[7]], [bank[2], bank[0]]], [bank[5], bank[6]], bank[3]
            nb = len(blocks)
            pulls = 1 if nb >= 64 else -(-64 // nb)
            info = {}

            def stA(i):
                info[i] = blocks[i]()
                kf = info[i][0]
                for hd in range(2):
                    O.mm(zb1[hd][i % 2](A_, ns), kf(hd), qfn(hd))

            def stB(i):
                mk, bi = info[i][2], info[i][3]
                for hd in range(2):
                    Pt = pb_[hd][i % 2]
                    O.act(Pt(A_, ns), zb1[hd][i % 2](A_, ns), AF.Exp, bias=btab(A_, slice(bi, bi + 1), hd))
                    if mk is not None:
                        O.tt("pool", Pt(A_, ns), Pt(A_, ns), mk, ALU.mult)

            def stE(i):
                vf = info[i][1]
                for hd in range(2):
                    O.mm(Ob1[hd](slice(0, 65), ns), vf(hd), pb_[hd][i % 2](A_, ns), start=(i == 0), stop=(i == nb - 1))

            la = 2
            for i0 in range(min(la, nb)):
                stA(i0)
            for i in range(nb):
                stB(i)
                if i + la < nb:
                    stA(i + la)
                stE(i)
                if side is not None:
                    for _ in range(pulls):
                        next(side, None)
            if side is not None:
                for _ in side:
                    pass
            for hd in range(2):
                O.act(oc_t[hd](slice(0, 65), ns), Ob1[hd](slice(0, 65), ns), AF.Copy)
                S.op("dve", lambda e, hd=hd: e.reciprocal(out=rlt[hd][64:65, :n], in_=oc_t[hd][64:65, :n]),
                     reads=[oc_t[hd]], writes=[rlt[hd]])
                O.mm(bcb(slice(0, 64), ns), onesf(slice(64, 65), slice(0, 64)), rlt[hd](slice(64, 65), ns))
                O.tt("dve", outs[hd](A_, slice(oc0, oc0 + n)), oc_t[hd](slice(0, 64), ns), bcb(slice(0, 64), ns), ALU.mult)

        def layer1_tile(src_ap, n, it, okT_ap, ov_ap, olf_ap, prompt_j):
            b = it % 2
            ns = slice(0, n)
            MBi = MB1[it]
            S.dma("sp", X1[:, :, :n], src_ap.rearrange("(kc p) n -> p kc n", p=128), X1, writes=[X1])
            O.act(Q1(A_, A_, ns), X1(A_, A_, ns), AF.Square)
            for kc in range(KC):
                O.mm(bank[0](A_, ns), ones_bf(), Q1(A_, kc, ns), start=(kc == 0), stop=(kc == KC - 1))
            O.act(L1(A_, ns), bank[0](A_, ns), AF.Ln, scale=1.0 / D, bias=EPS)
            O.act(R1(A_, ns), L1(A_, ns), AF.Exp, scale=-0.5)
            for kc in range(KC):
                O.stt(H1(A_, kc, ns), X1(A_, kc, ns), gm1(A_, slice(kc, kc + 1)), R1(A_, ns), ALU.mult, ALU.mult)
            H = H1
            proj_fm(H, n, wb, 0, bank[1])
            proj_fm(H, n, wb, 128, bank[2])
            proj_fm(H, n, wb, 384, bank[3])
            O.act(sigf(A_, ns), bank[2](A_, ns), AF.Sigmoid)
            O.act(qs(A_, ns), bank[1](A_, ns), AF.Silu)
            O.act(gs(A_, ns), bank[3](A_, ns), AF.Silu)
            O.ts("dve", fg(A_, ns), sigf(A_, ns), omlb(A_, slice(0, 1)), ALU.mult, lbcol(A_, slice(0, 1)), ALU.add)
            O.ts("dve", kk(A_, ns), fg(A_, ns), -1.0, ALU.mult, 1.0, ALU.add)
            O.act(lgf(A_, ns), fg(A_, ns), AF.Ln)
            if prompt_j is not None:
                chunks = [(64 * c, 64, 64 * (c % 2), None, 0) for c in range(n // 64)]
            else:
                chunks = [(DS * i, DS, 0, i, 1 + i) for i in range(NSB)]
            for (c0, L, po, sbi, seq) in chunks:
                O.scan(bcs(A_, slice(c0, c0 + L)), onesf(A_, slice(0, L)), lgf(A_, slice(c0, c0 + L)), 0.0)
            O.act(eb(A_, ns), bcs(A_, ns), AF.Exp)
            O.act(enb(A_, ns), bcs(A_, ns), AF.Exp, scale=-1.0)
            O.tt("dve", qt(A_, ns), qs(A_, ns), eb(A_, ns), ALU.mult)
            O.tt("dve", kt(A_, ns), kk(A_, ns), enb(A_, ns), ALU.mult)
            for ci, (c0, L, po, sbi, seq) in enumerate(chunks):
                e_ = slice(c0 + L - 1, c0 + L)
                O.act(exb(A_, slice(c0, c0 + L)), bcs(A_, slice(c0, c0 + L)), AF.Exp, scale=-1.0, bias=bcs(A_, e_))
                O.copy("pool", ebend(A_, slice(ci, ci + 1)), eb(A_, e_))
            O.tt("dve", khT(A_, ns), kk(A_, ns), exb(A_, ns), ALU.mult)
            if prompt_j is not None:
                proj_tm(H, n, wb, 256, bank[5])
                S.op("dve", lambda e: e.tensor_copy(out=vtm[:, :, :], in_=bank[5][:, :].rearrange("p (s c) -> p s c", c=128)),
                     reads=[bank[5]], writes=[vtm])
            else:
                for i in range(NSB):
                    for kc in range(KC):
                        O.mm(bank[5](slice(0, DS), slice(128 * i, 128 * i + 128)), H(A_, kc, slice(DS * i, DS * i + DS)),
                             wb(A_, kc, slice(256, 384)), start=(kc == 0), stop=(kc == KC - 1))
                    O.copy("dve", vtms[i](), bank[5](slice(0, DS), slice(128 * i, 128 * i + 128)))
            vfn = lambda c0, L, po, sbi: (vtm(slice(po, po + L), c0 // 128, A_) if sbi is None else vtms[sbi]())

            def hg_post():
                ob = bank[4]
                O.act(osq(A_, ns), ob(A_, ns), AF.Square)
                O.mm(bank[5](A_, ns), ones_bf(), osq(A_, ns))
                O.act(L1(A_, ns), bank[5](A_, ns), AF.Ln, scale=1.0 / 128, bias=EPS)
                O.act(R1(A_, ns), L1(A_, ns), AF.Exp, scale=-0.5)
                O.tt("dve", onn(A_, ns), ob(A_, ns), R1(A_, ns), ALU.mult)
                O.stt(hgo[b](A_, ns), onn(A_, ns), gncol, gs(A_, ns), ALU.mult, ALU.mult)
                S.dma("pool", MBi[0:128, 0:n], hgo[b][:, :n], hgo[b], reads=[hgo[b]])

            if prompt_j is not None:
                side = hg_gen(chunks, vfn, bank[3], bank[3], 128, bank[3], bank[4], sc0=256)
            else:
                side = None
                for _ in hg_gen(chunks, vfn, bank[1], bank[2], 0, bank[3], bank[4]):
                    pass
                hg_post()
            K = K1s[b]
            proj_fm(H, n, wb, 640, bank[1])
            O.act(K(A_, ns), bank[1](A_, ns), AF.Copy)
            O.st(okT_ap, K(A_, ns))
            proj_fm(H, n, wb, 512, bank[2])
            O.act(qT1(A_, ns), bank[2](A_, ns), AF.Copy, scale=0.125)
            ms = [mixfx[0][b], mixfx[1][b]]
            V = V1s[b]
            if prompt_j is not None:
                j = prompt_j
                proj_tm(H, n, wb, 768, bank[3])
                S.op("dve", lambda e: e.tensor_copy(out=V[:, :, :, :], in_=bank[3][:, :].rearrange("p (s h c) -> p s h c", h=2, c=64)),
                     reads=[bank[3]], writes=[V])
                S.dma("pool", ov_ap.rearrange("(s p) c -> p s c", p=128), V[:, :, :, :].rearrange("p s h c -> p s (h c)"), V, reads=[V])
                O.copy("pool", k1_all(A_, slice(j * NT, j * NT + n)), K(A_, ns))
                O.copy("pool", v1_all(A_, slice(4 * j, 4 * j + 4), A_, slice(0, 64)), V())
                for s_ in range(4):
                    for kc in range(KC):
                        O.mm(bank[7](A_, slice(16 + 2 * s_, 18 + 2 * s_)), H(A_, kc, slice(128 * s_, 128 * s_ + 128)),
                             wb(A_, kc, slice(896, 898)), start=(kc == 0), stop=(kc == KC - 1), skip=True)
                O.tt("dve", lft(), bank[7](A_, slice(16, 24)), bfr(), ALU.add)
                O.act(lft(), lft(), AF.Exp, scale=-1.0)
                O.act(lft(), lft(), AF.Ln, bias=1.0)
                LF = lfo[b]
                O.ts("dve", LF(), lft(), -1.0, ALU.mult)
                S.dma("pool", olf_ap.rearrange("(s p) h -> p s h", p=128), LF[:, :].rearrange("p (s h) -> p s h", h=2), LF, reads=[LF])
                for s_ in range(4):
                    if s_ == 2:
                        O.copy("dve", shq(), carry())
                    logf_block(LF(A_, slice(2 * s_, 2 * s_ + 2)), 128, cKM(A_, 4 * j + s_, A_), carry)
                nkb = 4 * j + 4
                for hd in range(2):
                    O.ts("dve", btab(A_, slice(0, nkb), hd), cKM(A_, slice(0, nkb), hd), -1.0, ALU.mult, shq(A_, slice(hd, hd + 1)), ALU.add)
                blocks = []
                for kb_ in range(nkb - 1, -1, -1):
                    m_ = kb_ - 4 * j
                    mk = masks_bf(A_, 1, slice(384 - 128 * m_, 896 - 128 * m_)) if m_ >= 0 else None
                    blocks.append(lambda kb_=kb_, mk=mk: (
                        lambda hd: k1_all(slice(64 * hd, 64 * hd + 64), slice(kb_ * 128, kb_ * 128 + 128)),
                        lambda hd: v1_all(A_, kb_, hd, A_), mk, kb_))
                fox_attend(lambda hd: qT1(slice(64 * hd, 64 * hd + 64), ns), n, blocks, ms, 0, side=side)
                hg_post()
            else:
                for i in range(NSB):
                    c0 = DS * i
                    cs = slice(c0, c0 + DS)
                    O.memset("pool", knew1[i](), 0.0)
                    O.memset("pool", vnew1[i](), 0.0)
                    O.copy("pool", knew1[i](A_, slice(0, DS)), K(A_, cs))
                    for kc in range(KC):
                        O.mm(bank[3](slice(0, DS), slice(0, 128)), H(A_, kc, cs), wb(A_, kc, slice(768, 896)),
                             start=(kc == 0), stop=(kc == KC - 1))
                    O.copy("dve", vfs(), bank[3](slice(0, DS), slice(0, 128)))
                    S.dma("pool", ov_ap[c0:c0 + DS, :], vfs[:, :], vfs, reads=[vfs])
                    S.op("pool", lambda e, i=i: e.tensor_copy(out=vnew1[i][0:DS, :, 0:64], in_=vfs[:, :].rearrange("p (h c) -> p h c", h=2)),
                         reads=[vfs], writes=[vnew1[i]])
                    O.memset("pool", vnew1[i](slice(0, DS), A_, slice(64, 65)), 1.0)
                    for kc in range(KC):
                        O.mm(bank[7](slice(0, DS), slice(16, 18)), H(A_, kc, cs), wb(A_, kc, slice(896, 898)),
                             start=(kc == 0), stop=(kc == KC - 1), skip=True)
                    O.tt("dve", lft(slice(0, DS), slice(0, 2)), bank[7](slice(0, DS), slice(16, 18)), bfr(slice(0, DS), slice(0, 2)), ALU.add)
                    O.act(lft(slice(0, DS), slice(0, 2)), lft(slice(0, DS), slice(0, 2)), AF.Exp, scale=-1.0)
                    O.act(lft(slice(0, DS), slice(0, 2)), lft(slice(0, DS), slice(0, 2)), AF.Ln, bias=1.0)
                    O.ts("dve", lfn[i](), lft(slice(0, DS), slice(0, 2)), -1.0, ALU.mult)
                    S.dma("pool", olf_ap[c0:c0 + DS, :], lfn[i][:, :], lfn[i], reads=[lfn[i]])
                    O.ld(flct(), flc[:, i, :, :])
                    O.memset("dve", carry(), 0.0)
                    O.memset("dve", cKMs(), 0.0)
                    for kb_ in range(NCB):
                        logf_block(flct(A_, kb_, A_), 128, cKMs(A_, kb_, A_), carry)
                    O.copy("dve", shq(), carry())
                    logf_block(lfn[i](), DS, cKMs(slice(0, DS), NCB, A_), carry)
                    for hd in range(2):
                        O.ts("dve", btab(A_, slice(0, NCB + 1), hd), cKMs(A_, slice(0, NCB + 1), hd), -1.0, ALU.mult,
                             shq(A_, slice(hd, hd + 1)), ALU.add)
                    blocks = [lambda i=i: (lambda hd: knew1[i](slice(64 * hd, 64 * hd + 64), A_),
                                           lambda hd: vnew1[i](A_, hd, A_),
                                           masks_bf(A_, 1, slice(384, 384 + DS)), NCB)]
                    for kb_ in range(NCB - 1, -1, -1):
                        def blk(kb_=kb_, i=i):
                            ci, sub = kb_ // 4, kb_ % 4
                            pr_ = ci % 2
                            if sub == 3:
                                O.ld(kcs1[pr_](), fkc[:, i, ci * 512:(ci + 1) * 512])
                                O.copy("pool", kcb1[pr_](), kcs1[pr_]())
                                S.dma("sp", vcs1[pr_][:, :, :, :].rearrange("p s h c -> p s (h c)"),
                                      fvc[ci * 512:(ci + 1) * 512, i, :].rearrange("(s p) c -> p s c", p=128), vcs1[pr_], writes=[vcs1[pr_]])
                                O.copy("pool", vcb1[pr_](A_, A_, A_, slice(0, 64)), vcs1[pr_]())
                            return (lambda hd: kcb1[pr_](slice(64 * hd, 64 * hd + 64), slice(sub * 128, sub * 128 + 128)),
                                    lambda hd: vcb1[pr_](A_, sub, hd, A_), None, kb_)
                        blocks.append(blk)
                    fox_attend(lambda hd, cs=cs: qT1(slice(64 * hd, 64 * hd + 64), cs), DS, blocks, ms, c0)
            for hd in range(2):
                S.dma("pool", MBi[128 + 64 * hd:192 + 64 * hd, 0:n], ms[hd][:, :n], ms[hd], reads=[ms[hd]])
            exchange(MBi, MG1[it], [hgo[b], ms[0], ms[1]])

        it = 0
        for j in range(T // NT):
            cl = slice(j * NT, (j + 1) * NT)
            layer1_tile(XS[:, cl], NT, it, o_fkT[:, cl], o_fv[cl, :], o_flf[cl, :], j)
            it += 1
        layer1_tile(XS[:, T:T + NS], NS, it, o_fkT_s[:, :], o_fv_s[:, :], o_flf_s[:, :], None)
        S.dma("sp", o_hg[:, :], Sst[:, :, :].rearrange("p s v -> p (s v)"), Sst, reads=[Sst])
        cut('l1mix')
        tmp_close()
        run_phaseC(1, MG1)

    except _Cut:
        pass
    S.finish()
    with nc.Block() as block:
        S.emit(block)
    while len(scopes) > 1:
        scopes.pop().close()
    es.close()
    return nc


_NC_CACHE = {}
OUT_NAMES = ["o_kT", "o_v", "o_kT_s", "o_v_s", "o_s5", "o_conv0", "o_conv1", "o_y", "o_fkT", "o_fv", "o_flf", "o_fkT_s", "o_fv_s", "o_flf_s", "o_hg"]


def _bc(row):
    return np.ascontiguousarray(np.broadcast_to(np.asarray(row, np.float32)[None, :], (128, row.shape[0])))


def kernel(**inp):
    f = np.float32
    x_prompt = np.asarray(inp["x_prompt"], f)
    x_sample = np.asarray(inp["x_sample"], f)
    B, T, _ = x_prompt.shape
    P = inp["cache_sb_k"].shape[2]
    key = (T, P)
    if key not in _NC_CACHE:
        _NC_CACHE[key] = build(T, P)
    nc = _NC_CACHE[key]
    g = lambda k: np.asarray(inp[k], f)

    in_maps = []
    for c in range(8):
        b, q = c // 4, c % 4
        m = {}
        m["xT_p"] = np.ascontiguousarray(x_prompt[b].T)
        m["xT_s"] = np.ascontiguousarray(x_sample[4 * b:4 * b + 4].reshape(NSB * DS, D).T)
        m["g_mix0"] = np.ascontiguousarray(g("norm_mix_g")[0].reshape(KC, 128).T)
        wa = g("w_in_a")[0]
        cols = np.concatenate([np.arange(128 * q, 128 * q + 128) + off for off in (0, 512, 1024, 1536)])
        m["w_in_a"] = np.ascontiguousarray(wa[:, cols])
        G = slice(8 * q, 8 * q + 8)
        lre, lim = g("s5_lambda_re")[0, G], g("s5_lambda_im")[0, G]
        ldt = np.repeat(g("s5_log_dt")[0, G][:, None], 64, axis=1)
        flat = lambda a: a.reshape(4, 128)
        m["s5col"] = np.ascontiguousarray(np.stack([flat(lre).T, flat(lim).T, flat(ldt).T], axis=1))
        m["s5row"] = np.ascontiguousarray(np.stack([_bc(lre.reshape(-1)), _bc(lim.reshape(-1)), _bc(ldt.reshape(-1))], axis=1))
        Bb = np.zeros((128, 2, 4, 128), f)
        Cb = np.zeros((128, 2, 4, 128), f)
        bre, bim = g("s5_b_re")[0, G], g("s5_b_im")[0, G]
        cre, cim = g("s5_c_re")[0, G], g("s5_c_im")[0, G]
        for gl in range(8):
            j, hf = gl // 2, gl % 2
            Bb[16 * gl:16 * gl + 16, 0, j, 64 * hf:64 * hf + 64] = bre[gl].T
            Bb[16 * gl:16 * gl + 16, 1, j, 64 * hf:64 * hf + 64] = bim[gl].T
            Cb[64 * hf:64 * hf + 64, 0, j, 16 * gl:16 * gl + 16] = cre[gl].T
            Cb[64 * hf:64 * hf + 64, 1, j, 16 * gl:16 * gl + 16] = cim[gl].T
        m["s5B"] = Bb.reshape(128, 2, 512)
        m["s5C"] = Cb.reshape(128, 2, 512)
        m["s5d"] = np.ascontiguousarray(g("s5_d")[0, 128 * q:128 * q + 128].reshape(128, 1))
        hre0 = g("state_s5_re")[0, 4 * b:4 * b + 4, G]
        him0 = g("state_s5_im")[0, 4 * b:4 * b + 4, G]
        h0 = np.stack([hre0.reshape(NSB, 4, 128), him0.reshape(NSB, 4, 128)], axis=0)
        m["s5h0"] = np.ascontiguousarray(h0.transpose(3, 0, 1, 2))
        jj = np.arange(128)[:, None]
        cc = np.arange(896)[None, :] - 384
        m["cmask"] = np.ascontiguousarray(np.stack([(jj < cc), (jj <= cc)], axis=1).astype(f))
        kk_ = np.arange(128)[None, :]
        m["ctri"] = np.ascontiguousarray(np.stack([(jj >= kk_), (jj < kk_), (jj <= kk_), (jj == kk_)], axis=1).astype(f))
        ck = g("cache_sb_k")[0, 4 * b:4 * b + 4, :, 2 * q:2 * q + 2, :]
        cv = g("cache_sb_v")[0, 4 * b:4 * b + 4, :, 2 * q:2 * q + 2, :]
        m["sbkc"] = np.ascontiguousarray(ck.transpose(2, 3, 0, 1).reshape(128, NSB, P))
        m["sbvc"] = np.ascontiguousarray(cv.transpose(1, 0, 2, 3).reshape(P, NSB, 128))
        perm = np.concatenate([np.concatenate([np.arange(128 * r, 128 * r + 128), 512 + np.arange(128 * r, 128 * r + 128)]) for r in range(4)])
        m["w_out0"] = np.ascontiguousarray(g("w_out_a")[0][perm, :])
        m["w_out1"] = np.ascontiguousarray(g("w_out_b")[0][perm, :])
        m["g_fin"] = np.ascontiguousarray(g("final_norm_g").reshape(KC, 128).T)
        m["w_glu"] = np.ascontiguousarray(g("s5_w_glu")[0])
        m["b_glu"] = np.ascontiguousarray(g("s5_b_glu")[0].reshape(4, 128).T)
        m["g_mix1"] = np.ascontiguousarray(g("norm_mix_g")[1].reshape(KC, 128).T)
        wbf = g("w_in_b")[0]
        cols1 = np.concatenate([np.arange(128 * q, 128 * q + 128) + off for off in (0, 512, 1024, 1536, 2048, 2560, 3072)]
                               + [np.array([3584 + 2 * q, 3584 + 2 * q + 1])])
        wpad = np.zeros((D, 1024), f)
        wpad[:, :898] = wbf[:, cols1]
        m["w_in_b"] = wpad
        lbl = g("hg_lb_logits")[:, 128 * q:128 * q + 128]
        m["hgp"] = np.ascontiguousarray(np.stack([lbl[0], lbl[1], g("hg_norm_g")[0, 128 * q:128 * q + 128]], axis=1))
        bf2 = g("fox_b_f")[0, 2 * q:2 * q + 2]
        m["bfrow"] = np.ascontiguousarray(np.broadcast_to(np.tile(bf2, 4)[None, :], (128, 8)))
        m["hgS0"] = np.ascontiguousarray(g("state_hgrn")[0, 4 * b:4 * b + 4, q].transpose(1, 0, 2))
        fk_ = g("cache_fox_k")[0, 4 * b:4 * b + 4, :, 2 * q:2 * q + 2, :]
        fv_ = g("cache_fox_v")[0, 4 * b:4 * b + 4, :, 2 * q:2 * q + 2, :]
        m["fkc"] = np.ascontiguousarray(fk_.transpose(2, 3, 0, 1).reshape(128, NSB, P))
        m["fvc"] = np.ascontiguousarray(fv_.transpose(1, 0, 2, 3).reshape(P, NSB, 128))
        fl_ = g("cache_fox_logf")[0, 4 * b:4 * b + 4, :, 2 * q:2 * q + 2]
        m["flc"] = np.ascontiguousarray(fl_.reshape(NSB, P // 128, 128, 2).transpose(2, 0, 1, 3))
        NFL_ = 6
        chs = [6 * q + s_ for s_ in range(NFL_)]
        for l in range(2):
            m[f"g_ffn{l}"] = np.ascontiguousarray(g("norm_ffn_g")[l].reshape(KC, 128).T)
            wu = g("ffn_w_up")[l]
            wd = g("ffn_w_down")[l]
            cw = np.concatenate([g("ffn_conv_w")[l], g("ffn_conv_b")[l][None, :]], axis=0)
            cs_ = g("state_ffn_conv")[l, 4 * b:4 * b + 4]
            wul = np.zeros((D, NFL_, 256), f)
            wdl = np.zeros((NFL_ * 128, D), f)
            cwl = np.zeros((128, NFL_, 4), f)
            csl = np.zeros((128, NSB, NFL_, 2), f)
            for s_, ch in enumerate(chs):
                if ch >= 22:
                    continue
                cl = slice(128 * ch, 128 * ch + 128)
                wul[:, s_, :128] = wu[:, cl]
                wul[:, s_, 128:] = wu[:, 2816 + 128 * ch:2816 + 128 * ch + 128]
                wdl[128 * s_:128 * s_ + 128] = wd[cl]
                cwl[:, s_, :] = cw[:, cl].T
                csl[:, :, s_, :] = cs_[:, :, cl].transpose(2, 0, 1)
            m[f"w_up{l}"] = wul
            m[f"w_dn{l}"] = wdl
            m[f"cw{l}"] = cwl
            m[f"cst{l}"] = csl
        in_maps.append(m)

    res = run_bass_kernel_spmd(nc, in_maps, core_ids=list(range(8)))
    R = res.results

    NE, NO = 1, 1
    p_sb_k = np.zeros((NE, B, T, 8, 64), f)
    p_sb_v = np.zeros((NE, B, T, 8, 64), f)
    s_sb_k = np.zeros((NE, 8, DS, 8, 64), f)
    s_sb_v = np.zeros((NE, 8, DS, 8, 64), f)
    p_s5 = np.zeros((2, NE, B, 32, 64), f)
    s_s5 = np.zeros((2, NE, 8, 32, 64), f)
    for c in range(8):
        b, q = c // 4, c % 4
        r = R[c]
        p_sb_k[0, b, :, 2 * q:2 * q + 2, :] = r["o_kT"].reshape(128, T).T.reshape(T, 2, 64)
        p_sb_v[0, b, :, 2 * q:2 * q + 2, :] = r["o_v"].reshape(T, 2, 64)
        s_sb_k[0, 4 * b:4 * b + 4, :, 2 * q:2 * q + 2, :] = r["o_kT_s"].reshape(128, NSB * DS).T.reshape(NSB, DS, 2, 64)
        s_sb_v[0, 4 * b:4 * b + 4, :, 2 * q:2 * q + 2, :] = r["o_v_s"].reshape(NSB, DS, 2, 64)
        st = r["o_s5"].reshape(128, 1 + NSB, 2, 4)
        st = st.transpose(1, 2, 3, 0).reshape(1 + NSB, 2, 8, 64)
        for ri in range(2):
            p_s5[ri, 0, b, 8 * q:8 * q + 8] = st[0, ri]
            s_s5[ri, 0, 4 * b:4 * b + 4, 8 * q:8 * q + 8] = st[1:, ri]

    p_fk = np.zeros((NO, B, T, 8, 64), f)
    p_fv = np.zeros((NO, B, T, 8, 64), f)
    p_fl = np.zeros((NO, B, T, 8), f)
    s_fk = np.zeros((NO, 8, DS, 8, 64), f)
    s_fv = np.zeros((NO, 8, DS, 8, 64), f)
    s_fl = np.zeros((NO, 8, DS, 8), f)
    p_hg = np.zeros((NO, B, 4, 128, 128), f)
    s_hg = np.zeros((NO, 8, 4, 128, 128), f)
    p_cv = np.zeros((2, B, 2, 2816), f)
    s_cv = np.zeros((2, 8, 2, 2816), f)
    y_p = np.zeros((B, T, D), f)
    y_s = np.zeros((8, DS, D), f)
    NS_ = NSB * DS
    for c in range(8):
        b, q = c // 4, c % 4
        r = R[c]
        p_fk[0, b, :, 2 * q:2 * q + 2, :] = r["o_fkT"].reshape(128, T).T.reshape(T, 2, 64)
        p_fv[0, b, :, 2 * q:2 * q + 2, :] = r["o_fv"].reshape(T, 2, 64)
        p_fl[0, b, :, 2 * q:2 * q + 2] = r["o_flf"].reshape(T, 2)
        s_fk[0, 4 * b:4 * b + 4, :, 2 * q:2 * q + 2, :] = r["o_fkT_s"].reshape(128, NS_).T.reshape(NSB, DS, 2, 64)
        s_fv[0, 4 * b:4 * b + 4, :, 2 * q:2 * q + 2, :] = r["o_fv_s"].reshape(NSB, DS, 2, 64)
        s_fl[0, 4 * b:4 * b + 4, :, 2 * q:2 * q + 2] = r["o_flf_s"].reshape(NSB, DS, 2)
        hg = r["o_hg"].reshape(128, 1 + NSB, 128).transpose(1, 0, 2)
        p_hg[0, b, q] = hg[0]
        s_hg[0, 4 * b:4 * b + 4, q] = hg[1:]
        for l in range(2):
            ac = r[f"o_conv{l}"].reshape(128, 1 + NSB, 6, 2).transpose(1, 3, 2, 0)
            for s_ in range(6):
                ch = 6 * q + s_
                if ch < 22:
                    p_cv[l, b, :, 128 * ch:128 * ch + 128] = ac[0, :, s_, :]
                    s_cv[l, 4 * b:4 * b + 4, :, 128 * ch:128 * ch + 128] = ac[1:, :, s_, :]
        if q == 0:
            yT = r["o_y"].reshape(D, T + NS_)
            y_p[b] = yT[:, :T].T
            y_s[4 * b:4 * b + 4] = yT[:, T:].T.reshape(NSB, DS, D)

    return (y_p, y_s,
            p_sb_k, p_sb_v, p_s5[0], p_s5[1], p_fk, p_fv, p_fl, p_hg, p_cv,
            s_sb_k, s_sb_v, s_s5[0], s_s5[1], s_fk, s_fv, s_fl, s_hg, s_cv)
```
